# Optimizing a Trainium2 kernel written in Bass

```python
import jax, jax.numpy as jnp
from jax import lax
import numpy as np

D_MODEL = 1024
BATCH = 32
SEQ = 2048
DEPTH = 1

GRID_W = 64
CTX_LEN = 256
HEAD_DIM = 128
N_Q_HEADS = D_MODEL // HEAD_DIM
N_KV_HEADS = N_Q_HEADS // 4
Q_GROUP = N_Q_HEADS // N_KV_HEADS
ATTN_WIDTH = N_Q_HEADS * HEAD_DIM
KV_WIDTH = N_KV_HEADS * HEAD_DIM
POOL_WINDOWS = (2, 4, 8, 16)
N_POOL_GROUPS = len(POOL_WINDOWS)
POOL_WIDTH = D_MODEL // 2
POOL_GROUP_WIDTH = POOL_WIDTH // N_POOL_GROUPS
N_BRANCHES = 2
IN_WIDTH = ATTN_WIDTH + 2 * KV_WIDTH + POOL_WIDTH + N_BRANCHES * D_MODEL
SPLIT_POINTS = (ATTN_WIDTH, ATTN_WIDTH + KV_WIDTH, ATTN_WIDTH + 2 * KV_WIDTH,
                ATTN_WIDTH + 2 * KV_WIDTH + POOL_WIDTH)
D_FF = 4 * D_MODEL
Q_BLOCK = 128
ROPE_THETA = 10000.0
EPS = 1e-6
N_MOD = 6

kernel_name = "hybrid_gated_gqa_pool_dit_block"


def rms_norm(x, g):
    xf = x.astype(jnp.float32)
    y = xf * lax.rsqrt(jnp.mean(xf * xf, axis=-1, keepdims=True) + EPS)
    return (y * g.astype(jnp.float32)).astype(x.dtype)


def adaln(cond, w_mod, b_mod):
    m = jax.nn.silu(cond) @ w_mod + b_mod
    return jnp.split(m, N_MOD, axis=-1)


def modulate(h, shift, scale):
    return h * (1 + scale) + shift


def rotate_half_pairs(xp, ang):
    f = ang.shape[-1]
    x1, x2 = xp[..., :f], xp[..., f:]
    cos, sin = jnp.cos(ang), jnp.sin(ang)
    return jnp.concatenate([x1 * cos - x2 * sin, x1 * sin + x2 * cos], axis=-1)


def axial_rope(x, ang_row, ang_col):
    xf = x.astype(jnp.float32)
    half = HEAD_DIM // 2
    out = jnp.concatenate([rotate_half_pairs(xf[..., :half], ang_row),
                           rotate_half_pairs(xf[..., half:], ang_col)], axis=-1)
    return out.astype(x.dtype)


def to_heads(t, n_heads):
    b, n, _ = t.shape
    return t.reshape(b, n, n_heads, HEAD_DIM).transpose(0, 2, 1, 3)


def from_heads(t):
    b, h, n, d = t.shape
    return t.transpose(0, 2, 1, 3).reshape(b, n, h * d)


def split_projection(h, w_in, g_q, g_k):
    p = h @ w_in
    q, k, v, pool_in, gate_logits = jnp.split(p, SPLIT_POINTS, axis=-1)
    q = rms_norm(to_heads(q, N_Q_HEADS), g_q)
    k = rms_norm(to_heads(k, N_KV_HEADS), g_k)
    v = to_heads(v, N_KV_HEADS)
    return q, k, v, pool_in, gate_logits


def gqa_softmax(qg, k, v):
    s = jnp.einsum('bkgqd,bkmd->bkgqm', qg, k, preferred_element_type=jnp.float32)
    p = jax.nn.softmax(s * (HEAD_DIM ** -0.5), axis=-1)
    return jnp.einsum('bkgqm,bkmd->bkgqd', p.astype(v.dtype), v)


def latent_attention(q, k_lat, v_lat, k_ctx, v_ctx):
    b, _, n, _ = q.shape
    k_all = jnp.concatenate([k_ctx, k_lat], axis=2)
    v_all = jnp.concatenate([v_ctx, v_lat], axis=2)
    n_blk = n // Q_BLOCK
    qb = q.reshape(b, N_KV_HEADS, Q_GROUP, n_blk, Q_BLOCK, HEAD_DIM)
    qb = jnp.moveaxis(qb, 3, 0)
    o = lax.map(lambda blk: gqa_softmax(blk, k_all, v_all), qb)
    o = jnp.moveaxis(o, 0, 3).reshape(b, N_Q_HEADS, n, HEAD_DIM)
    return from_heads(o)


def context_attention(q, k, v):
    b, _, n, _ = q.shape
    qg = q.reshape(b, N_KV_HEADS, Q_GROUP, n, HEAD_DIM)
    o = gqa_softmax(qg, k, v).reshape(b, N_Q_HEADS, n, HEAD_DIM)
    return from_heads(o)


def multiscale_pool(u, w_grp, scale):
    b, n, _ = u.shape
    uf = u.astype(jnp.float32)
    cs = jnp.concatenate([jnp.zeros((b, 1, POOL_WIDTH), jnp.float32),
                          jnp.cumsum(uf, axis=1)], axis=1)
    t = jnp.arange(n)
    outs = []
    for gi, w in enumerate(POOL_WINDOWS):
        lo = jnp.clip(t - w // 2, 0, n)
        hi = jnp.clip(t + w // 2, 0, n)
        sl = slice(gi * POOL_GROUP_WIDTH, (gi + 1) * POOL_GROUP_WIDTH)
        csg = cs[..., sl]
        cnt = (hi - lo).astype(jnp.float32)[:, None]
        mean = (jnp.take(csg, hi, axis=1) - jnp.take(csg, lo, axis=1)) / cnt
        outs.append(mean - uf[..., sl])
    pooled = jnp.stack(outs, axis=2).astype(u.dtype)
    mixed = jnp.einsum('btgc,gcd->btgd', pooled, w_grp).reshape(b, n, POOL_WIDTH)
    return mixed * scale


def merge_branches(attn_o, pool_o, gate_logits, w_attn_up, w_pool_up, w_out):
    ya = attn_o @ w_attn_up
    yp = pool_o @ w_pool_up
    gates = jax.nn.sigmoid(gate_logits.astype(jnp.float32)).astype(ya.dtype)
    ga, gp = jnp.split(gates, N_BRANCHES, axis=-1)
    return (ga * ya + gp * yp) @ w_out


def sq_relu_mlp(h, w_ff1, w_ff2):
    return jnp.square(jax.nn.relu(h @ w_ff1)) @ w_ff2


def setup_inputs(seed: int = 0) -> dict:
    key = jax.random.key(seed)
    ks = jax.random.split(key, 24)
    f32 = jnp.float32

    def w(k, shape, fan_in):
        return jax.random.normal(k, shape, f32) * (fan_in ** -0.5)

    def gain(k, shape):
        return 1.0 + 0.05 * jax.random.normal(k, shape, f32)

    def bias(k, shape):
        return 0.02 * jax.random.normal(k, shape, f32)

    L = DEPTH
    return {
        "x": jax.random.normal(ks[0], (BATCH, SEQ, D_MODEL), f32),
        "c": jax.random.normal(ks[1], (BATCH, D_MODEL), f32),
        "ctx": jax.random.normal(ks[2], (BATCH, CTX_LEN, D_MODEL), f32),
        "c_ctx": jax.random.normal(ks[3], (D_MODEL,), f32),
        "w_mod": w(ks[4], (L, D_MODEL, N_MOD * D_MODEL), D_MODEL),
        "b_mod": bias(ks[5], (L, N_MOD * D_MODEL)),
        "g_pre_mix": gain(ks[6], (L, D_MODEL)),
        "g_post_mix": gain(ks[7], (L, D_MODEL)),
        "g_pre_mlp": gain(ks[8], (L, D_MODEL)),
        "g_post_mlp": gain(ks[9], (L, D_MODEL)),
        "w_in": w(ks[10], (L, D_MODEL, IN_WIDTH), D_MODEL),
        "b_gate": bias(ks[11], (L, N_BRANCHES * D_MODEL)),
        "g_q": gain(ks[12], (L, HEAD_DIM)),
        "g_k": gain(ks[13], (L, HEAD_DIM)),
        "w_attn_up": w(ks[14], (L, ATTN_WIDTH, D_MODEL), ATTN_WIDTH),
        "w_pool_grp": w(ks[15], (L, N_POOL_GROUPS, POOL_GROUP_WIDTH, POOL_GROUP_WIDTH), POOL_GROUP_WIDTH),
        "pool_scale": gain(ks[16], (L, POOL_WIDTH)),
        "w_pool_up": w(ks[17], (L, POOL_WIDTH, D_MODEL), POOL_WIDTH),
        "w_out": w(ks[18], (L, D_MODEL, D_MODEL), D_MODEL),
        "w_ff1": w(ks[19], (L, D_MODEL, D_FF), D_MODEL),
        "w_ff2": w(ks[20], (L, D_FF, D_MODEL), D_FF),
    }


def reference(x, c, ctx, c_ctx, w_mod, b_mod, g_pre_mix, g_post_mix, g_pre_mlp, g_post_mlp,
              w_in, b_gate, g_q, g_k, w_attn_up, w_pool_grp, pool_scale, w_pool_up, w_out,
              w_ff1, w_ff2):
    n_lat = x.shape[1]
    ROWS = n_lat // GRID_W
    rows = jnp.repeat(jnp.arange(ROWS), GRID_W).astype(jnp.float32)
    cols = jnp.tile(jnp.arange(GRID_W), ROWS).astype(jnp.float32)
    n_freq = HEAD_DIM // 4
    freqs = ROPE_THETA ** (-jnp.arange(n_freq, dtype=jnp.float32) / n_freq)
    ang_row = rows[:, None] * freqs
    ang_col = cols[:, None] * freqs

    for i in range(DEPTH):
        last = i == DEPTH - 1
        sh1, sc1, ga1, sh2, sc2, ga2 = [m[:, None, :] for m in adaln(c, w_mod[i], b_mod[i])]
        csh1, csc1, cga1, csh2, csc2, cga2 = adaln(c_ctx, w_mod[i], b_mod[i])

        w_in_i = w_in[i]
        h_lat = modulate(rms_norm(x, g_pre_mix[i]), sh1, sc1)
        h_ctx = modulate(rms_norm(ctx, g_pre_mix[i]), csh1, csc1)
        q_l, k_l, v_l, pool_l, gl_l = split_projection(h_lat, w_in_i, g_q[i], g_k[i])
        q_c, k_c, v_c, pool_c, gl_c = split_projection(h_ctx, w_in_i, g_q[i], g_k[i])
        gl_l = gl_l + b_gate[i]
        gl_c = gl_c + b_gate[i]
        q_l = axial_rope(q_l, ang_row, ang_col)
        k_l = axial_rope(k_l, ang_row, ang_col)

        attn_l = latent_attention(q_l, k_l, v_l, k_c, v_c)
        pooled_l = multiscale_pool(pool_l, w_pool_grp[i], pool_scale[i])
        y_l = merge_branches(attn_l, pooled_l, gl_l, w_attn_up[i], w_pool_up[i], w_out[i])
        x = x + ga1 * rms_norm(y_l, g_post_mix[i])

        if not last:
            attn_c = context_attention(q_c, k_c, v_c)
            pooled_c = multiscale_pool(pool_c, w_pool_grp[i], pool_scale[i])
            y_c = merge_branches(attn_c, pooled_c, gl_c, w_attn_up[i], w_pool_up[i], w_out[i])
            ctx = ctx + cga1 * rms_norm(y_c, g_post_mix[i])

        h2 = modulate(rms_norm(x, g_pre_mlp[i]), sh2, sc2)
        x = x + ga2 * rms_norm(sq_relu_mlp(h2, w_ff1[i], w_ff2[i]), g_post_mlp[i])
        if not last:
            h2c = modulate(rms_norm(ctx, g_pre_mlp[i]), csh2, csc2)
            ctx = ctx + cga2 * rms_norm(sq_relu_mlp(h2c, w_ff1[i], w_ff2[i]), g_post_mlp[i])

    return x
```

```python
import math
from contextlib import ExitStack

import numpy as np
import ml_dtypes

import concourse.bass as bass
import concourse.mybir as mybir
from concourse.bass_utils import run_bass_kernel_spmd

F32 = mybir.dt.float32
BF16 = mybir.dt.bfloat16
I32 = mybir.dt.int32
AF = mybir.ActivationFunctionType
ALU = mybir.AluOpType
AX = mybir.AxisListType

N_CORES = 8
D = 1024
S = 2048
CTXL = 256
HD = 128
NQH = 8
NKV = 2
INW = 4096
DFF = 4096
NLT = S // 128
NCT = CTXL // 128
NKT = NLT + NCT
NKEY = NKT * 128
EPS = 1e-6
POOL_WINDOWS = (2, 4, 8, 16)

ENGINES = ("pe", "act", "dve", "pool", "sp")
SEM_ROT = 30000
DMA_RING = {"sp": 12, "pool": 8}


class _Op:
    __slots__ = ("eng", "fn", "dma", "deps", "signal", "ticket")

    def __init__(self, eng, fn, dma):
        self.eng = eng
        self.fn = fn
        self.dma = dma
        self.deps = []
        self.signal = False
        self.ticket = None


class Tracker:
    def __init__(self):
        self.ops = {e: [] for e in ENGINES}
        self.last_w = {}
        self.readers = {}
        self.dma_ops = {e: [] for e in DMA_RING}

    def add(self, eng, fn, reads=(), writes=(), dma=False):
        op = _Op(eng, fn, dma)
        deps = {}

        def need(d, kind):
            if d is None or d is op:
                return
            if (not d.dma) and (not dma) and d.eng == eng:
                if eng == "pe":
                    return
            deps[id(d)] = d

        for r in reads:
            need(self.last_w.get(r), "raw")
            if isinstance(r, tuple) and r and r[0] == "pb":
                rd = self.readers.get(r)
                if rd:
                    for k, v in rd.items():
                        if k != "_dma" and k != eng:
                            need(v, "rar")
        for w in writes:
            need(self.last_w.get(w), "waw")
            rd = self.readers.get(w)
            if rd:
                for k, v in rd.items():
                    if k == "_dma":
                        for d in v:
                            need(d, "war")
                    else:
                        need(v, "war")
        if dma:
            lst = self.dma_ops[eng]
            ring = DMA_RING[eng]
            if len(lst) >= ring:
                d = lst[len(lst) - ring]
                deps[id(d)] = d
            lst.append(op)
        op.deps = list(deps.values())
        for d in op.deps:
            d.signal = True
        for r in reads:
            rd = self.readers.setdefault(r, {})
            if dma:
                rd.setdefault("_dma", []).append(op)
            else:
                rd[eng] = op
        for w in writes:
            self.last_w[w] = op
            self.readers[w] = {}
        self.ops[eng].append(op)
        return op

    def finalize(self):
        need = {}
        for e in ENGINES:
            c = 0
            for op in self.ops[e]:
                if op.dma:
                    continue
                if op.signal:
                    c += 1
                    op.ticket = (e, (c - 1) // SEM_ROT, (c - 1) % SEM_ROT + 1)
            need[e] = (c + SEM_ROT - 1) // SEM_ROT
        self.dma_final = {}
        for e, lst in self.dma_ops.items():
            ring = DMA_RING[e]
            cnt = [0] * ring
            for i, op in enumerate(lst):
                s = i % ring
                cnt[s] += 1
                op.ticket = ("dma_" + e, s, 16 * cnt[s])
            self.dma_final[e] = cnt
        return need


def emit_program(nc, trk, stack):
    need = trk.finalize()
    sems = {}
    for e in ENGINES:
        for r in range(need[e]):
            sems[(e, r)] = stack.enter_context(nc.semaphore(f"pg_{e}_{r}"))
    for e, ring in DMA_RING.items():
        for s in range(ring):
            sems[("dma_" + e, s)] = stack.enter_context(nc.semaphore(f"dq_{e}_{s}"))
    block = stack.enter_context(nc.Block())

    def body(eng_name):
        def run(eng):
            waited = {}
            for op in trk.ops[eng_name]:
                for d in op.deps:
                    k, r, v = d.ticket
                    key = (k, r)
                    if waited.get(key, 0) >= v:
                        continue
                    waited[key] = v
                    eng.wait_ge(sems[key], v)
                ins = op.fn(eng)
                k, r, v = op.ticket if op.ticket is not None else (None, None, None)
                if op.dma:
                    ins.then_inc(sems[(k, r)], 16)
                elif op.signal:
                    ins.then_inc(sems[(k, r)], 1)
            if eng_name in trk.dma_final:
                for s_, c in enumerate(trk.dma_final[eng_name]):
                    if c:
                        eng.wait_ge(sems[("dma_" + eng_name, s_)], 16 * c)
        return run

    reg = {"pe": block.tensor, "act": block.scalar, "dve": block.vector,
           "pool": block.gpsimd, "sp": block.sync}
    for e in ENGINES:
        if trk.ops[e]:
            reg[e](body(e))


def _band_consts():
    out = np.zeros((4, 5, 128, 128), np.float32)
    n = S
    for gi, w in enumerate(POOL_WINDOWS):
        def blk(dst_tile, src_tile):
            m = np.zeros((128, 128), np.float32)
            for tl in range(128):
                t = dst_tile * 128 + tl
                lo = min(max(t - w // 2, 0), n)
                hi = min(max(t + w // 2, 0), n)
                cnt = float(hi - lo)
                for s_ in range(lo, hi):
                    sl = s_ - src_tile * 128
                    if 0 <= sl < 128:
                        m[sl, tl] += 1.0 / cnt
                if src_tile == dst_tile:
                    m[tl, tl] -= 1.0
            return m
        out[gi, 0] = blk(5, 5)
        out[gi, 1] = blk(0, 0)
        out[gi, 2] = blk(NLT - 1, NLT - 1)
        out[gi, 3] = blk(5, 4)
        out[gi, 4] = blk(5, 6)
    return np.ascontiguousarray(out.reshape(20, 128, 128).transpose(1, 0, 2)).astype(ml_dtypes.bfloat16)


class _Stop(Exception):
    pass


def build_program(NB, stop_after=None, skip=()):
    nc = bass.Bass("TRN2", target_bir_lowering=False)
    NV = NB + 1

    def din(name, shape, dt=F32):
        return nc.dram_tensor(name, list(shape), dt, kind="ExternalInput").ap()

    x_d = din("x", [NB, S, D])
    ctx_d = din("ctx", [NB, CTXL, D])
    c_d = din("c", [NB, D])
    cctx_d = din("c_ctx", [D])
    wmod_d = din("w_mod", [D, 6 * D])
    bmod_d = din("b_mod", [6 * D])
    gpm_d = din("g_pre_mix", [D])
    gpo_d = din("g_post_mix", [D])
    gpl_d = din("g_pre_mlp", [D])
    gpol_d = din("g_post_mlp", [D])
    win_d = din("w_in", [D, INW])
    bgate_d = din("b_gate", [2 * D])
    gq_d = din("g_q", [HD])
    gk_d = din("g_k", [HD])
    wau_d = din("w_attn_up", [D, D])
    wgrp_d = din("w_pool_grp", [4 * 128, 128])
    psc_d = din("pool_scale", [512])
    wpu_d = din("w_pool_up", [512, D])
    wo_d = din("w_out", [D, D])
    w1_d = din("w_ff1", [D, DFF])
    w2_d = din("w_ff2", [DFF, D])
    ident_d = din("k_ident", [128, 128], BF16)
    bands_d = din("k_bands", [128, 20, 128], BF16)
    out_d = nc.dram_tensor("out", [NB, S, D], F32, kind="ExternalOutput").ap()

    def dsc(name, shape, dt=BF16):
        return nc.dram_tensor(name, list(shape), dt).ap()

    win_b = dsc("win_b", [D, INW])
    wau_b = dsc("wau_b", [D, D])
    wpu_b = dsc("wpu_b", [512, D])
    wgrp_b = dsc("wgrp_b", [512, 128])
    wo_b = dsc("wo_b", [D, D])
    w1_b = dsc("w1_b", [D, DFF])
    w2_b = dsc("w2_b", [DFF, D])
    modsc = dsc("modsc", [NV, 6, D], F32)

    T = Tracker()

    def A(eng, meth, *args, r=(), w=(), **kw):
        T.add(eng, lambda e: getattr(e, meth)(*args, **kw), reads=r, writes=w)

    def DMA(out, in_, r=(), w=(), eng="sp", **kw):
        T.add(eng, lambda e: e.dma_start(out=out, in_=in_, **kw), reads=r, writes=w, dma=True)

    with ExitStack() as st:
        def sb(name, shape, dt):
            return st.enter_context(nc.sbuf_tensor(name, list(shape), dt))

        ident = sb("ident", [128, 128], BF16)
        identf = sb("identf", [128, 128], F32)
        onesb = sb("onesb", [128, 128], BF16)
        bands = sb("bands", [128, 20, 128], BF16)
        cosT = sb("cosT", [128, NLT, 64], F32)
        sinT = sb("sinT", [128, NLT, 64], F32)
        gqb = sb("gqb", [128, 128], F32)
        gkb = sb("gkb", [128, 128], F32)
        bgc = sb("bgc", [128, 16], F32)
        psc = sb("psc", [128, 4], F32)
        epst = sb("epst", [128, 1], F32)
        wgrp = sb("wgrp", [128, 4, 128], BF16)
        AD = sb("AD", [128, 6, D], F32)
        KT = sb("KT", [128, NKV, NKEY], BF16)
        VV = sb("VV", [128, NKT, 256], BF16)
        UU = sb("UU", [128, NLT, 512], BF16)
        XT = sb("XT", [128, 2, D], F32)
        HT = sb("HT", [128, 8, 512], BF16)
        RR = sb("RR", [128, 32, 512], BF16)
        PT = sb("PT", [128, 2, 1024], BF16)
        PO = sb("PO", [128, 4, 512], BF16)
        ZZ = sb("ZZ", [128, 8, 512], BF16)
        X1 = sb("X1", [128, 4, D], F32)
        TF = sb("TF", [128, 2, D], F32)
        HB = sb("HB", [128, 2, D], BF16)
        QN = sb("QN", [128, D], F32)
        QR = sb("QR", [128, D], BF16)
        RL = sb("RL", [128, 2, 512], BF16)
        RD = sb("RD", [128, 512], F32)
        JK = sb("JK", [128, D], BF16)
        WS = sb("WS", [128, 3, 4096], BF16)
        STT = sb("STT", [128, 256], F32)
        SMALL = sb("SMALL", [128, 64], F32)
        SMALLI = sb("SMALLI", [128, 64], I32)
        SMALLR = sb("SMALLR", [16, 256], F32)

        ps01 = st.enter_context(nc.psum_tensor("ps01", [128, 1024], F32))
        ps23 = st.enter_context(nc.psum_tensor("ps23", [128, 1024], F32))
        ps4 = st.enter_context(nc.psum_tensor("ps4", [128, 512], F32))
        ps5 = st.enter_context(nc.psum_tensor("ps5", [128, 512], F32))
        ps6 = st.enter_context(nc.psum_tensor("ps6", [128, 512], F32))
        ps7 = st.enter_context(nc.psum_tensor("ps7", [128, 512], F32))

        def bank(i):
            if i == 0:
                return ps01[:, 0:512]
            if i == 1:
                return ps01[:, 512:1024]
            if i == 2:
                return ps23[:, 0:512]
            if i == 3:
                return ps23[:, 512:1024]
            return (ps4, ps5, ps6, ps7)[i - 4][:]

        def bank2(i):
            return (ps01, ps23)[i][:]

        def pb(i):
            return ("pb", i)

        QTv = RR[:, 0:8, :]
        AOv = RR[:, 8:16, :]
        GTv = RR[:, 16:32, :]
        YAv = ZZ[:].rearrange("p a b -> p (a b)").bitcast(F32).rearrange("p (i c) -> p i c", c=512)

        def ya_keys(i):
            return [("Z", 2 * i + kk, jj) for kk in range(2) for jj in range(4)]

        PDv = PT[:].rearrange("p a b -> p (a b)").rearrange("p (g t) -> p g t", t=512)
        PTK = [("PT", 0), ("PT", 1)]

        stt_ctr = [0]

        def stat(n):
            if stt_ctr[0] + n > 256:
                stt_ctr[0] = 0
            c0 = stt_ctr[0]
            stt_ctr[0] += n
            return STT[:, c0:c0 + n], [("STT", c) for c in range(c0, c0 + n)]

        X1K = [("X1", i) for i in range(4)]

        def chk(n):
            if stop_after is not None and stop_after == n:
                raise _Stop()


        try:
            DMA(ident[:], ident_d[:, :], w=["ident"])
            DMA(bands[:], bands_d[:, :, :], w=["bands"])
            DMA(gqb[:], gq_d.partition_broadcast(128), w=["gqb"])
            DMA(gkb[:], gk_d.partition_broadcast(128), w=["gkb"])
            DMA(SMALLR[0:16, 0:128], bgate_d.rearrange("(c p) -> c p", p=128), w=["smallr_a"])
            DMA(SMALLR[0:4, 128:256], psc_d.rearrange("(c p) -> c p", p=128), w=["smallr_b"])
            A("pool", "memset", onesb[:], 1.0, w=["onesb"])
            A("pool", "memset", epst[:], EPS, w=["epst"])
            A("dve", "tensor_copy", out=identf[:], in_=ident[:], r=["ident"], w=["identf"])
            A("pe", "transpose", out=bank(4)[:, 0:16], in_=SMALLR[0:16, 0:128], identity=identf[0:16, 0:16],
              r=["smallr_a", "identf"], w=[pb(4)])
            A("pe", "transpose", out=bank(4)[:, 16:20], in_=SMALLR[0:4, 128:256], identity=identf[0:4, 0:4],
              r=["smallr_b", "identf"], w=[pb(4)])
            A("dve", "tensor_copy", out=bgc[:], in_=bank(4)[:, 0:16], r=[pb(4)], w=["bgc"])
            A("dve", "tensor_copy", out=psc[:], in_=bank(4)[:, 16:20], r=[pb(4)], w=["psc"])

            pidx = SMALLI[:, 0:1]
            fidx = SMALLI[:, 1:33]
            ridx = SMALLI[:, 33:49]
            pf = SMALL[:, 0:1]
            hi_ = SMALL[:, 1:2]
            colp = SMALL[:, 2:3]
            freq = SMALL[:, 3:35]
            rowp = SMALL[:, 35:51]
            A("pool", "iota", pidx, pattern=[[0, 1]], base=0, channel_multiplier=1, w=["pidx"])
            A("pool", "iota", fidx, pattern=[[1, 32]], base=0, channel_multiplier=0, w=["fidx"])
            A("pool", "iota", ridx, pattern=[[2, 16]], base=0, channel_multiplier=0, w=["ridx"])
            A("dve", "tensor_copy", out=pf, in_=pidx, r=["pidx"], w=["pf"])
            A("dve", "tensor_single_scalar", out=hi_, in_=pf, scalar=64.0, op=ALU.is_ge, r=["pf"], w=["hi"])
            A("dve", "scalar_tensor_tensor", out=colp, in0=hi_, scalar=-64.0, in1=pf, op0=ALU.mult, op1=ALU.add,
              r=["hi", "pf"], w=["colp"])
            A("dve", "tensor_copy", out=freq, in_=fidx, r=["fidx"], w=["freq"])
            A("act", "activation", out=freq, in_=freq, func=AF.Exp, scale=-math.log(10000.0) / 32.0, r=["freq"], w=["freq"])
            A("dve", "tensor_copy", out=rowp, in_=ridx, r=["ridx"], w=["rowp"])
            A("dve", "tensor_scalar", out=rowp, in0=rowp, scalar1=hi_, scalar2=None, op0=ALU.add, r=["rowp", "hi"], w=["rowp"])
            ANG = X1[:, 0, :].rearrange("p (j f) -> p j f", f=64)
            ANGF = X1[:, 0, :]
            KKI = X1[:, 1, :].bitcast(I32)
            KKF = X1[:, 2, :]
            RRD = X1[:, 3, :]
            A("dve", "tensor_tensor", out=ANG[:, :, 0:32], in0=rowp.unsqueeze(2).broadcast_to([128, 16, 32]),
              in1=freq.unsqueeze(1).broadcast_to([128, 16, 32]), op=ALU.mult, r=["rowp", "freq"], w=[("X1", 0)])
            A("dve", "tensor_scalar", out=ANG[:, :, 32:64], in0=freq.unsqueeze(1).broadcast_to([128, 16, 32]),
              scalar1=colp, scalar2=None, op0=ALU.mult, r=["colp", "freq", ("X1", 0)], w=[("X1", 0)])
            for which, shift, dstT in (("sinT", 0.0, sinT), ("cosT", math.pi / 2.0, cosT)):
                A("dve", "tensor_scalar", out=KKI, in0=ANGF, scalar1=shift, scalar2=1.0 / (2 * math.pi),
                  op0=ALU.add, op1=ALU.mult, r=[("X1", 0)], w=[("X1", 1)])
                A("dve", "tensor_copy", out=KKF, in_=KKI, r=[("X1", 1)], w=[("X1", 2)])
                A("dve", "scalar_tensor_tensor", out=RRD, in0=KKF, scalar=-2.0 * math.pi, in1=ANGF,
                  op0=ALU.mult, op1=ALU.add, r=[("X1", 2), ("X1", 0)], w=[("X1", 3)])
                A("dve", "tensor_scalar", out=RRD, in0=RRD, scalar1=shift, scalar2=-math.pi, op0=ALU.add, op1=ALU.max,
                  r=[("X1", 3)], w=[("X1", 3)])
                A("dve", "tensor_scalar", out=RRD, in0=RRD, scalar1=math.pi, scalar2=None, op0=ALU.min,
                  r=[("X1", 3)], w=[("X1", 3)])
                A("act", "activation", out=dstT[:].rearrange("p j f -> p (j f)"), in_=RRD, func=AF.Sin,
                  r=[("X1", 3)], w=[which])

            chk(0)
            def cast(dst, src, rows, name):
                for rb in range(rows // 128):
                    DMA(dst[rb * 128:(rb + 1) * 128, :], src[rb * 128:(rb + 1) * 128, :], w=[("wsc", name, rb)], eng="pool")

            cast(win_b, win_d, D, "win")
            cast(wgrp_b, wgrp_d, 512, "wgrp")
            cast(wau_b, wau_d, D, "wau")
            cast(wpu_b, wpu_d, 512, "wpu")
            cast(wo_b, wo_d, D, "wo")
            cast(w1_b, w1_d, D, "w1")
            cast(w2_b, w2_d, DFF, "w2")
            DMA(wgrp[:], wgrp_b.rearrange("(g c) d -> c g d", c=128), r=[("wsc", "wgrp", rb) for rb in range(4)], w=["wgrp"])

            chk(1)
            crow = TF[0:NV, 0, :]
            srow = TF[0:NV, 1, :]
            DMA(TF[0:NB, 0, :], c_d[:, :], w=[("TF", 0)])
            DMA(TF[NB:NV, 0, :], cctx_d.partition_broadcast(1), w=[("TF", 0)])
            A("act", "activation", out=srow, in_=crow, func=AF.Silu, r=[("TF", 0)], w=[("TF", 1)])
            scT = QN[:, 0:8 * NV].rearrange("p (k v) -> p k v", v=NV)
            for k in range(8):
                A("pe", "transpose", out=bank(4)[:, k * NV:(k + 1) * NV], in_=srow[:, k * 128:(k + 1) * 128],
                  identity=identf[0:NV, 0:NV], r=[("TF", 1), "identf"], w=[pb(4)])
            A("dve", "tensor_copy", out=QN[:, 0:8 * NV], in_=bank(4)[:, 0:8 * NV], r=[pb(4)], w=["QN"])
            WMv = [RR[:, 0:16, :].rearrange("p a b -> p (a b)").bitcast(F32).rearrange("p (k n) -> p k n", n=512),
                   RR[:, 16:32, :].rearrange("p a b -> p (a b)").bitcast(F32).rearrange("p (k n) -> p k n", n=512)]
            WMk = [[("R", f) for f in range(0, 16)], [("R", f) for f in range(16, 32)]]
            gvecs = {1: gpm_d, 2: gpo_d, 4: gpl_d, 5: gpol_d}
            wmod_v = wmod_d.rearrange("(k p) n -> p k n", p=128)
            for n in range(12):
                sl = n % 2
                j, half = n // 2, n % 2
                DMA(WMv[sl], wmod_v[:, :, n * 512:(n + 1) * 512], w=WMk[sl])
                brow = HB[0:NV, sl, :].bitcast(F32)
                mrow = XT[0:NV, sl, 0:512]
                g_t = XT[0:NV, sl, 512:1024]
                DMA(brow, bmod_d[n * 512:(n + 1) * 512].partition_broadcast(NV), w=[("HBk", sl)])
                if j in gvecs:
                    DMA(g_t, gvecs[j][half * 512:(half + 1) * 512].partition_broadcast(NV), w=[("XT", sl)])
                for k in range(8):
                    A("pe", "matmul", bank(5)[0:NV, :], scT[:, k, :], WMv[sl][:, k, :], start=(k == 0), stop=(k == 7),
                      r=["QN"] + WMk[sl], w=[pb(5)])
                A("dve", "tensor_tensor", out=mrow, in0=bank(5)[0:NV, :], in1=brow, op=ALU.add,
                  r=[pb(5), ("HBk", sl)], w=[("XT", sl)])
                if j in (1, 4):
                    A("dve", "scalar_tensor_tensor", out=mrow, in0=mrow, scalar=1.0, in1=g_t, op0=ALU.add, op1=ALU.mult,
                      r=[("XT", sl)], w=[("XT", sl)])
                elif j in (2, 5):
                    A("dve", "tensor_tensor", out=mrow, in0=mrow, in1=g_t, op=ALU.mult, r=[("XT", sl)], w=[("XT", sl)])
                DMA(modsc[:, j, half * 512:(half + 1) * 512], mrow, r=[("XT", sl)], w=[("modsc", n)])
            modsc_keys = [("modsc", n) for n in range(12)]

            chk(2)
            ws_ctr = [0]

            def wload(src_ap, ncols, rkeys):
                slot = ws_ctr[0] % 3
                ws_ctr[0] += 1
                view = WS[:, slot, :].rearrange("p (k n) -> p k n", n=ncols)
                DMA(view, src_ap, r=rkeys, w=[("WS", slot)])
                return view, ("WS", slot)

            def wkeys(name, rbs):
                return [("wsc", name, rb) for rb in rbs]

            win_v = win_b.rearrange("(k p) n -> p k n", p=128)
            wau_v = wau_b.rearrange("(k p) n -> p k n", p=128)
            wpu_v = wpu_b.rearrange("(k p) n -> p k n", p=128)
            wo_v = wo_b.rearrange("(k p) n -> p k n", p=128)
            w1_v = w1_b.rearrange("(k p) n -> p k n", p=128)
            w2_v = w2_b.rearrange("(k p) n -> p k n", p=128)

            def rstd_from_ss(ss, ssk, dim):
                sd, sdk = stat(ss.shape[1])
                rs, rsk = stat(ss.shape[1])
                A("act", "activation", out=sd, in_=ss, func=AF.Sqrt, bias=epst[:, 0:1], scale=1.0 / dim,
                  r=ssk + ["epst"], w=sdk)
                A("dve", "reciprocal", out=rs, in_=sd, r=sdk, w=rsk)
                return rs, rsk

            def norm_part1(src, src_keys, Aj, Bj, sl):
                ss, ssk = stat(1)
                A("act", "activation", out=JK[:], in_=src, func=AF.Square, accum_out=ss, r=src_keys, w=ssk + ["JK"])
                rs, rsk = rstd_from_ss(ss, ssk, D)
                A("dve", "scalar_tensor_tensor", out=TF[:, sl, :], in0=src, scalar=rs, in1=AD[:, Aj, :],
                  op0=ALU.mult, op1=ALU.mult, r=src_keys + rsk + [("AD", Aj)], w=[("TF", sl)])
                A("pool", "tensor_tensor", out=HB[:, sl, :], in0=TF[:, sl, :], in1=AD[:, Bj, :], op=ALU.add,
                  r=[("TF", sl), ("AD", Bj)], w=[("HBk", sl)])

            def norm_part2(sl, i):
                tpv = bank(6).bitcast(BF16)
                for k in range(8):
                    A("pe", "transpose", out=tpv[:, k * 128:(k + 1) * 128], in_=HB[:, sl, k * 128:(k + 1) * 128],
                      identity=ident[:], r=[("HBk", sl), "ident"], w=[pb(6)])
                A("act", "activation", out=HT[:, :, i * 128:(i + 1) * 128], in_=tpv.rearrange("p (k t) -> p k t", t=128),
                  func=AF.Copy, r=[pb(6)], w=[("HT", k, i) for k in range(8)])

            def qk_norm_rope(src_ap, src_keys, nh, gb, gbk, rope_j, dst_ap, dst_keys):
                W = nh * 128
                qn = QN[:, 0:W]
                qn3 = qn.rearrange("p (h d) -> p h d", d=128)
                t1 = TF[:, 0, 0:W]
                t13 = t1.rearrange("p (h d) -> p h d", d=128)
                t2 = TF[:, 1, 0:W]
                t23 = t2.rearrange("p (h d) -> p h d", d=128)
                qr = QR[:, 0:W]
                qr3 = qr.rearrange("p (h d) -> p h d", d=128)
                ssq, ssqk = stat(nh)
                A("act", "activation", out=qn, in_=src_ap, func=AF.Copy, r=src_keys, w=["QN"])
                A("dve", "tensor_tensor", out=t1, in0=qn, in1=qn, op=ALU.mult, r=["QN"], w=[("TF", 0)])
                A("dve", "tensor_reduce", out=ssq, in_=t13, axis=AX.X, op=ALU.add, r=[("TF", 0)], w=ssqk)
                rsq, rsqk = rstd_from_ss(ssq, ssqk, HD)
                A("dve", "tensor_tensor", out=qn3, in0=qn3, in1=rsq.unsqueeze(2).broadcast_to([128, nh, 128]), op=ALU.mult,
                  r=["QN"] + rsqk, w=["QN"])
                gbb = gb[:].unsqueeze(1).broadcast_to([128, nh, 128])
                if rope_j is None:
                    A("pool", "tensor_tensor", out=qr3, in0=qn3, in1=gbb, op=ALU.mult, r=["QN", gbk], w=["QR"])
                else:
                    A("pool", "tensor_tensor", out=qn3, in0=qn3, in1=gbb, op=ALU.mult, r=["QN", gbk], w=["QN"])
                    for rc in range(2):
                        o = rc * 64
                        cosb = cosT[:, rope_j, rc * 32:(rc + 1) * 32]
                        sinb = sinT[:, rope_j, rc * 32:(rc + 1) * 32]
                        qv = qn3[:, :, o:o + 64].rearrange("p h (a f) -> p h a f", a=2)
                        t1v = t13[:, :, o:o + 64].rearrange("p h (a f) -> p h a f", a=2)
                        A("dve", "tensor_tensor", out=t1v, in0=qv,
                          in1=cosb.unsqueeze(1).unsqueeze(2).broadcast_to([128, nh, 2, 32]), op=ALU.mult,
                          r=["QN", "cosT"], w=[("TF", 0)])
                        sb_ = sinb.unsqueeze(1).broadcast_to([128, nh, 32])
                        A("pool", "tensor_tensor", out=t23[:, :, o:o + 32], in0=qn3[:, :, o + 32:o + 64], in1=sb_, op=ALU.mult,
                          r=["QN", "sinT"], w=[("TF", 1)])
                        A("pool", "tensor_tensor", out=t23[:, :, o + 32:o + 64], in0=qn3[:, :, o:o + 32], in1=sb_, op=ALU.mult,
                          r=["QN", "sinT"], w=[("TF", 1)])
                    for half, op_, eng in ((0, ALU.subtract, "dve"), (1, ALU.add, "pool")):
                        def hv(t3):
                            return t3.rearrange("p h (r a f) -> p h r a f", r=2, a=2)[:, :, :, half, :]
                        A(eng, "tensor_tensor", out=hv(qr3), in0=hv(t13), in1=hv(t23), op=op_,
                          r=[("TF", 0), ("TF", 1)], w=["QR"])
                tpv = bank(7).bitcast(BF16)
                for hh in range(nh):
                    A("pe", "transpose", out=tpv[:, hh * 128:(hh + 1) * 128], in_=qr[:, hh * 128:(hh + 1) * 128],
                      identity=ident[:], r=["QR", "ident"], w=[pb(7)])
                A("dve", "tensor_copy", out=dst_ap, in_=tpv[:, 0:W].rearrange("p (h t) -> p h t", t=128),
                  r=[pb(7)], w=dst_keys)

            for b in range(NB):
                DMA(AD[:, 0:2, :], modsc[NB, 0:2, :].partition_broadcast(128), r=modsc_keys, w=[("AD", 0), ("AD", 1)])
                wkv, wkv_k = wload(win_v[:, :, 1024:1536], 512, wkeys("win", range(8)))
                wu, wu_k = wload(win_v[:, :, 1536:2048], 512, wkeys("win", range(8)))

                def a_load(t):
                    sl = t % 2
                    if t < NCT:
                        src = ctx_d[b, t * 128:(t + 1) * 128, :]
                    else:
                        src = x_d[b, (t - NCT) * 128:(t - NCT + 1) * 128, :]
                    DMA(XT[:, sl, :], src, w=[("XT", sl)])

                def a_n1(t):
                    norm_part1(XT[:, t % 2, :], [("XT", t % 2)], 1, 0, t % 2)

                def a_n2(t):
                    norm_part2(t % 2, t % 4)

                def a_proj(t):
                    i = t % 4
                    kvb = t % 2
                    ub = 2 + t % 2
                    for k in range(8):
                        A("pe", "matmul", bank(kvb), HT[:, k, i * 128:(i + 1) * 128], wkv[:, k, :],
                          start=(k == 0), stop=(k == 7), r=[("HT", k, i), wkv_k], w=[pb(kvb)])
                    if t >= NCT:
                        for k in range(8):
                            A("pe", "matmul", bank(ub), HT[:, k, i * 128:(i + 1) * 128], wu[:, k, :],
                              start=(k == 0), stop=(k == 7), r=[("HT", k, i), wu_k], w=[pb(ub)])

                def a_post(t):
                    kvb = t % 2
                    ub = 2 + t % 2
                    A("act", "activation", out=VV[:, t, :], in_=bank(kvb)[:, 256:512], func=AF.Copy, r=[pb(kvb)], w=[("V", t)])
                    if t >= NCT:
                        A("act", "activation", out=UU[:, t - NCT, :], in_=bank(ub), func=AF.Copy, r=[pb(ub)], w=[("U", t - NCT)])
                    qk_norm_rope(bank(kvb)[:, 0:256], [pb(kvb)], 2, gkb, "gkb", (t - NCT) if t >= NCT else None,
                                 KT[:, :, t * 128:(t + 1) * 128], [("KT", t)])

                a_load(0)
                a_load(1)
                a_n1(0)
                a_n2(0)
                for t in range(NKT):
                    if t + 1 < NKT:
                        if t + 1 == NCT:
                            DMA(AD[:], modsc[b, :, :].partition_broadcast(128), r=modsc_keys, w=[("AD", j) for j in range(6)])
                        a_n1(t + 1)
                    if t + 2 < NKT:
                        a_load(t + 2)
                    a_proj(t)
                    if t + 1 < NKT:
                        a_n2(t + 1)
                    a_post(t)

                chk(3)
                for s in range(4):
                    def xsrc(i):
                        j = 4 * s + i
                        return x_d[b, j * 128:(j + 1) * 128, :]

                    def b_load(i):
                        DMA(XT[:, i % 2, :], xsrc(i), w=[("XT", i % 2)])

                    b_load(0)
                    b_load(1)
                    wq0, wq0_k = wload(win_v[:, :, 0:512], 512, wkeys("win", range(8)))
                    wq1, wq1_k = wload(win_v[:, :, 512:1024], 512, wkeys("win", range(8)))
                    for i in range(4):
                        norm_part1(XT[:, i % 2, :], [("XT", i % 2)], 1, 0, i % 2)
                        if i + 2 < 4:
                            b_load(i + 2)
                        norm_part2(i % 2, i)

                    chk(4)
                    for i in range(4):
                        pr = i % 2
                        for c_, (wq, wqk) in enumerate(((wq0, wq0_k), (wq1, wq1_k))):
                            for k in range(8):
                                A("pe", "matmul", bank(2 * pr + c_), HT[:, k, i * 128:(i + 1) * 128], wq[:, k, :],
                                  start=(k == 0), stop=(k == 7), r=[("HT", k, i), wqk], w=[pb(2 * pr + c_)])
                        qk_norm_rope(bank2(pr), [pb(2 * pr), pb(2 * pr + 1)], 8, gqb, "gqb", 4 * s + i,
                                     QTv[:, :, i * 128:(i + 1) * 128], [("R", h) for h in range(8)])

                    chk(5)
                    HTall = [("HT", k, ii) for k in range(8) for ii in range(4)]
                    for c_ in range(4):
                        wg, wg_k = wload(win_v[:, :, 2048 + c_ * 512:2048 + (c_ + 1) * 512], 512, wkeys("win", range(8)))
                        for mm in range(4):
                            m = c_ * 4 + mm
                            gb_ = 4 + (m % 2)
                            for k in range(8):
                                A("pe", "matmul", bank(gb_), wg[:, k, mm * 128:(mm + 1) * 128], HT[:, k, :],
                                  start=(k == 0), stop=(k == 7), r=[("HT", k, ii) for ii in range(4)] + [wg_k], w=[pb(gb_)])
                            A("act", "activation", out=GTv[:, m, :], in_=bank(gb_), func=AF.Sigmoid, bias=bgc[:, m:m + 1],
                              scale=1.0, r=[pb(gb_), "bgc"], w=[("R", 16 + m)])

                    chk(6)
                    steps = [(h, pr) for h in range(NQH) for pr in range(NKT // 2)]
                    sc = 1.0 / math.sqrt(HD)

                    def qk(idx):
                        h, pr = steps[idx]
                        g = h // 4
                        slot = idx % 2
                        for u in range(2):
                            kt = 2 * pr + u
                            A("pe", "matmul", bank(2 * slot + u), KT[:, g, kt * 128:(kt + 1) * 128], QTv[:, h, :],
                              start=True, stop=True, r=[("KT", kt), ("R", h)], w=[pb(2 * slot + u)])

                    def ex(idx):
                        slot = idx % 2
                        A("act", "activation", out=PT[:, slot, :], in_=bank2(slot), func=AF.Exp, scale=sc,
                          r=[pb(2 * slot), pb(2 * slot + 1)], w=[("PT", slot)])

                    def pv(idx):
                        h, pr = steps[idx]
                        g = h // 4
                        slot = idx % 2
                        ob = 4 + h % 2
                        db = 6 + h % 2
                        for u in range(2):
                            kt = 2 * pr + u
                            first = (pr == 0 and u == 0)
                            last = (pr == NKT // 2 - 1 and u == 1)
                            A("pe", "matmul", bank(ob), VV[:, kt, g * 128:(g + 1) * 128], PT[:, slot, u * 512:(u + 1) * 512],
                              start=first, stop=last, r=[("V", kt), ("PT", slot)], w=[pb(ob)])
                            A("pe", "matmul", bank(db), onesb[:], PT[:, slot, u * 512:(u + 1) * 512],
                              start=first, stop=last, r=["onesb", ("PT", slot)], w=[pb(db)])
                        if pr == NKT // 2 - 1:
                            A("dve", "reciprocal", out=RD[:], in_=bank(db), r=[pb(db)], w=["RD"])
                            A("dve", "tensor_tensor", out=AOv[:, h, :], in0=bank(ob), in1=RD[:], op=ALU.mult,
                              r=[pb(ob), "RD"], w=[("R", 8 + h)])

                    qk(0)
                    for idx in range(len(steps)):
                        if idx + 1 < len(steps):
                            qk(idx + 1)
                        ex(idx)
                        pv(idx)

                    chk(7)
                    for i in range(4):
                        j = 4 * s + i
                        for g in range(4):
                            srcs = [(j - 1, 3), (j, 1 if j == 0 else (2 if j == NLT - 1 else 0)), (j + 1, 4)]
                            srcs = [(sj, v) for sj, v in srcs if 0 <= sj < NLT]
                            for n_, (sj, v) in enumerate(srcs):
                                A("pe", "matmul", bank(i)[:, g * 128:(g + 1) * 128], UU[:, sj, g * 128:(g + 1) * 128],
                                  bands[:, g * 5 + v, :], start=(n_ == 0), stop=(n_ == len(srcs) - 1),
                                  r=[("U", sj), "bands"], w=[pb(i)])
                        A("act", "activation", out=PDv[:, :, i * 128:(i + 1) * 128],
                          in_=bank(i).rearrange("p (g t) -> p g t", t=128), func=AF.Copy, r=[pb(i)], w=PTK)
                    for g in range(4):
                        gb_ = 4 + g % 2
                        A("pe", "matmul", bank(gb_), wgrp[:, g, :], PDv[:, g, :], start=True, stop=True,
                          r=["wgrp"] + PTK, w=[pb(gb_)])
                        A("act", "activation", out=PO[:, g, :], in_=bank(gb_), func=AF.Copy, scale=psc[:, g:g + 1],
                          r=[pb(gb_), "psc"], w=[("PO", g)])

                    chk(8)
                    wpu, wpu_k = wload(wpu_v[:, :, :], 1024, wkeys("wpu", range(4)))
                    wau_c, wau_ck = wload(wau_v[:, :, 0:512], 512, wkeys("wau", range(8)))
                    for m in range(8):
                        c_, mm = m // 4, m % 4
                        if m == 4:
                            wau_c, wau_ck = wload(wau_v[:, :, 512:1024], 512, wkeys("wau", range(8)))
                        par = m % 2
                        yab, ypb = 2 * par, 2 * par + 1
                        for h in range(8):
                            A("pe", "matmul", bank(yab), wau_c[:, h, mm * 128:(mm + 1) * 128], AOv[:, h, :],
                              start=(h == 0), stop=(h == 7), r=[("R", 8 + h), wau_ck], w=[pb(yab)])
                        for g in range(4):
                            A("pe", "matmul", bank(ypb), wpu[:, g, m * 128:(m + 1) * 128], PO[:, g, :],
                              start=(g == 0), stop=(g == 3), r=[("PO", g), wpu_k], w=[pb(ypb)])
                        t1 = TF[:, par, 0:512]
                        t2 = TF[:, par, 512:1024]
                        A("dve", "tensor_tensor", out=t1, in0=bank(yab), in1=GTv[:, m, :], op=ALU.mult,
                          r=[pb(yab), ("R", 16 + m)], w=[("TF", par)])
                        A("dve", "tensor_tensor", out=t2, in0=bank(ypb), in1=GTv[:, 8 + m, :], op=ALU.mult,
                          r=[pb(ypb), ("R", 24 + m)], w=[("TF", par)])
                        A("pool", "tensor_tensor", out=ZZ[:, m, :], in0=t1, in1=t2, op=ALU.add,
                          r=[("TF", par)], w=[("Z", m, ii) for ii in range(4)])

                    wo0, wo0_k = wload(wo_v[:, :, 0:512], 512, wkeys("wo", range(8)))
                    wo1, wo1_k = wload(wo_v[:, :, 512:1024], 512, wkeys("wo", range(8)))
                    for i in range(4):
                        pr = i % 2
                        sl = i % 2
                        DMA(XT[:, sl, :], xsrc(i), w=[("XT", sl)])
                        for c_, (wo_, wok) in enumerate(((wo0, wo0_k), (wo1, wo1_k))):
                            for k in range(8):
                                A("pe", "matmul", bank(2 * pr + c_), ZZ[:, k, i * 128:(i + 1) * 128], wo_[:, k, :],
                                  start=(k == 0), stop=(k == 7), r=[("Z", k, i), wok], w=[pb(2 * pr + c_)])
                        pk = [pb(2 * pr), pb(2 * pr + 1)]
                        ss, ssk = stat(1)
                        A("act", "activation", out=JK[:], in_=bank2(pr), func=AF.Square, accum_out=ss, r=pk, w=ssk + ["JK"])
                        rs, rsk = rstd_from_ss(ss, ssk, D)
                        A("dve", "scalar_tensor_tensor", out=TF[:, sl, :], in0=bank2(pr), scalar=rs, in1=AD[:, 2, :],
                          op0=ALU.mult, op1=ALU.mult, r=pk + [("AD", 2)] + rsk, w=[("TF", sl)])
                        A("pool", "tensor_tensor", out=X1[:, i, :], in0=TF[:, sl, :], in1=XT[:, sl, :], op=ALU.add,
                          r=[("TF", sl), ("XT", sl)], w=[("X1", i)])
                    for i in range(4):
                        norm_part1(X1[:, i, :], [("X1", i)], 4, 3, i % 2)
                        norm_part2(i % 2, i)

                    chk(9)
                    aTv = RR
                    for c_ in range(8):
                        w1c, w1c_k = wload(w1_v[:, :, c_ * 512:(c_ + 1) * 512], 512, wkeys("w1", range(8)))
                        for ff in range(4):
                            f = 4 * c_ + ff
                            fb = 4 + f % 2
                            for k in range(8):
                                A("pe", "matmul", bank(fb), w1c[:, k, ff * 128:(ff + 1) * 128], HT[:, k, :],
                                  start=(k == 0), stop=(k == 7), r=[("HT", k, ii) for ii in range(4)] + [w1c_k], w=[pb(fb)])
                            A("act", "activation", out=RL[:, f % 2, :], in_=bank(fb), func=AF.Relu, r=[pb(fb)], w=[("RL", f % 2)])
                            A("pool" if f % 2 == 0 else "dve", "tensor_tensor", out=aTv[:, f, :], in0=RL[:, f % 2, :],
                              in1=RL[:, f % 2, :], op=ALU.mult, r=[("RL", f % 2)], w=[("R", f)])
                    chk(10)
                    ssA, ssAk = stat(4)
                    ssB, ssBk = stat(4)
                    for half in range(2):
                        for pc in range(4):
                            w2p, w2p_k = wload(w2_v[:, 8 * pc:8 * pc + 8, half * 512:(half + 1) * 512], 512,
                                               wkeys("w2", range(8 * pc, 8 * pc + 8)))
                            for i in range(4):
                                for fk in range(8):
                                    f = 8 * pc + fk
                                    A("pe", "matmul", bank(i), aTv[:, f, i * 128:(i + 1) * 128], w2p[:, fk, :],
                                      start=(pc == 0 and fk == 0), stop=(pc == 3 and fk == 7), r=[("R", f), w2p_k], w=[pb(i)])
                        if half == 0:
                            for i in range(4):
                                A("dve", "tensor_copy", out=YAv[:, i, :], in_=bank(i), r=[pb(i)], w=ya_keys(i))
                                A("act", "activation", out=JK[:, 0:512], in_=YAv[:, i, :], func=AF.Square,
                                  accum_out=ssA[:, i:i + 1], r=ya_keys(i), w=[ssAk[i], "JK"])
                            chk(11)
                        else:
                            for i in range(4):
                                j = 4 * s + i
                                sl = i % 2
                                A("act", "activation", out=JK[:, 0:512], in_=bank(i), func=AF.Square, accum_out=ssB[:, i:i + 1],
                                  r=[pb(i)], w=[ssBk[i], "JK"])
                                A("dve", "tensor_tensor", out=ssB[:, i:i + 1], in0=ssB[:, i:i + 1], in1=ssA[:, i:i + 1], op=ALU.add,
                                  r=[ssAk[i], ssBk[i]], w=[ssBk[i]])
                                rs, rsk = rstd_from_ss(ssB[:, i:i + 1], [ssBk[i]], D)
                                A("dve", "scalar_tensor_tensor", out=TF[:, sl, 0:512], in0=YAv[:, i, :], scalar=rs,
                                  in1=AD[:, 5, 0:512], op0=ALU.mult, op1=ALU.mult, r=ya_keys(i) + rsk + [("AD", 5)], w=[("TF", sl)])
                                A("dve", "scalar_tensor_tensor", out=TF[:, sl, 512:1024], in0=bank(i), scalar=rs,
                                  in1=AD[:, 5, 512:1024], op0=ALU.mult, op1=ALU.mult, r=[pb(i), ("AD", 5)] + rsk, w=[("TF", sl)])
                                A("pool", "tensor_tensor", out=X1[:, i, :], in0=X1[:, i, :], in1=TF[:, sl, :], op=ALU.add,
                                  r=[("X1", i), ("TF", sl)], w=[("X1", i)])
                                DMA(out_d[b, j * 128:(j + 1) * 128, :], X1[:, i, :], r=[("X1", i)], w=[("out", b, j)])
                    chk(12)

        except _Stop:
            pass
        emit_program(nc, T, st)
    return nc


_CACHE = {}


def _get_program(NB):
    if NB not in _CACHE:
        _CACHE[NB] = build_program(NB)
    return _CACHE[NB]


def kernel(x, c, ctx, c_ctx, w_mod, b_mod, g_pre_mix, g_post_mix, g_pre_mlp, g_post_mlp,
           w_in, b_gate, g_q, g_k, w_attn_up, w_pool_grp, pool_scale, w_pool_up, w_out,
           w_ff1, w_ff2):
    f = lambda a: np.ascontiguousarray(np.asarray(a, dtype=np.float32))
    x = f(x); c = f(c); ctx = f(ctx)
    B = x.shape[0]
    NB = B // N_CORES
    nc = _get_program(NB)
    shared = {
        "c_ctx": f(c_ctx), "w_mod": f(w_mod)[0], "b_mod": f(b_mod)[0],
        "g_pre_mix": f(g_pre_mix)[0], "g_post_mix": f(g_post_mix)[0],
        "g_pre_mlp": f(g_pre_mlp)[0], "g_post_mlp": f(g_post_mlp)[0],
        "w_in": f(w_in)[0], "b_gate": f(b_gate)[0], "g_q": f(g_q)[0], "g_k": f(g_k)[0],
        "w_attn_up": f(w_attn_up)[0], "w_pool_grp": f(w_pool_grp)[0].reshape(512, 128),
        "pool_scale": f(pool_scale)[0], "w_pool_up": f(w_pool_up)[0], "w_out": f(w_out)[0],
        "w_ff1": f(w_ff1)[0], "w_ff2": f(w_ff2)[0],
        "k_ident": np.eye(128, dtype=np.float32).astype(ml_dtypes.bfloat16),
        "k_bands": _band_consts(),
    }
    in_maps = []
    for i in range(N_CORES):
        m = dict(shared)
        m["x"] = x[i * NB:(i + 1) * NB]
        m["c"] = c[i * NB:(i + 1) * NB]
        m["ctx"] = ctx[i * NB:(i + 1) * NB]
        in_maps.append(m)
    res = run_bass_kernel_spmd(nc, in_maps, core_ids=list(range(N_CORES)))
    out = np.concatenate([np.asarray(r["out"]) for r in res.results], axis=0)
    return out.astype(np.float32)
```

```python
import math
from contextlib import ExitStack

import numpy as np
import ml_dtypes

import concourse.bass as bass
import concourse.mybir as mybir
from concourse.bass_utils import run_bass_kernel_spmd

F32 = mybir.dt.float32
BF16 = mybir.dt.bfloat16
I32 = mybir.dt.int32
AF = mybir.ActivationFunctionType
ALU = mybir.AluOpType
AX = mybir.AxisListType

N_CORES = 8
D = 1024
S = 2048
CTXL = 256
HD = 128
NQH = 8
NKV = 2
INW = 4096
DFF = 4096
NLT = S // 128
NCT = CTXL // 128
NKT = NLT + NCT
NKEY = NKT * 128
EPS = 1e-6
POOL_WINDOWS = (2, 4, 8, 16)

ENGINES = ("pe", "act", "dve", "pool", "sp")
SEM_ROT = 30000
DMA_RING = {"sp": 12, "pool": 8}


class _Op:
    __slots__ = ("eng", "fn", "dma", "deps", "signal", "ticket")

    def __init__(self, eng, fn, dma):
        self.eng = eng
        self.fn = fn
        self.dma = dma
        self.deps = []
        self.signal = False
        self.ticket = None


class Tracker:
    def __init__(self):
        self.ops = {e: [] for e in ENGINES}
        self.last_w = {}
        self.readers = {}
        self.dma_ops = {e: [] for e in DMA_RING}

    def add(self, eng, fn, reads=(), writes=(), dma=False):
        op = _Op(eng, fn, dma)
        deps = {}

        def need(d, kind):
            if d is None or d is op:
                return
            if (not d.dma) and (not dma) and d.eng == eng:
                if eng == "pe":
                    return
            deps[id(d)] = d

        for r in reads:
            need(self.last_w.get(r), "raw")
            if isinstance(r, tuple) and r and r[0] == "pb":
                rd = self.readers.get(r)
                if rd:
                    for k, v in rd.items():
                        if k != "_dma" and k != eng:
                            need(v, "rar")
        for w in writes:
            need(self.last_w.get(w), "waw")
            rd = self.readers.get(w)
            if rd:
                for k, v in rd.items():
                    if k == "_dma":
                        for d in v:
                            need(d, "war")
                    else:
                        need(v, "war")
        if dma:
            lst = self.dma_ops[eng]
            ring = DMA_RING[eng]
            if len(lst) >= ring:
                d = lst[len(lst) - ring]
                deps[id(d)] = d
            lst.append(op)
        op.deps = list(deps.values())
        for d in op.deps:
            d.signal = True
        for r in reads:
            rd = self.readers.setdefault(r, {})
            if dma:
                rd.setdefault("_dma", []).append(op)
            else:
                rd[eng] = op
        for w in writes:
            self.last_w[w] = op
            self.readers[w] = {}
        self.ops[eng].append(op)
        return op

    def finalize(self):
        need = {}
        for e in ENGINES:
            c = 0
            for op in self.ops[e]:
                if op.dma:
                    continue
                if op.signal:
                    c += 1
                    op.ticket = (e, (c - 1) // SEM_ROT, (c - 1) % SEM_ROT + 1)
            need[e] = (c + SEM_ROT - 1) // SEM_ROT
        self.dma_final = {}
        for e, lst in self.dma_ops.items():
            ring = DMA_RING[e]
            cnt = [0] * ring
            for i, op in enumerate(lst):
                s = i % ring
                cnt[s] += 1
                op.ticket = ("dma_" + e, s, 16 * cnt[s])
            self.dma_final[e] = cnt
        return need


def emit_program(nc, trk, stack):
    need = trk.finalize()
    sems = {}
    for e in ENGINES:
        for r in range(need[e]):
            sems[(e, r)] = stack.enter_context(nc.semaphore(f"pg_{e}_{r}"))
    for e, ring in DMA_RING.items():
        for s in range(ring):
            sems[("dma_" + e, s)] = stack.enter_context(nc.semaphore(f"dq_{e}_{s}"))
    block = stack.enter_context(nc.Block())

    def body(eng_name):
        def run(eng):
            waited = {}
            for op in trk.ops[eng_name]:
                for d in op.deps:
                    k, r, v = d.ticket
                    key = (k, r)
                    if waited.get(key, 0) >= v:
                        continue
                    waited[key] = v
                    eng.wait_ge(sems[key], v)
                ins = op.fn(eng)
                k, r, v = op.ticket if op.ticket is not None else (None, None, None)
                if op.dma:
                    ins.then_inc(sems[(k, r)], 16)
                elif op.signal:
                    ins.then_inc(sems[(k, r)], 1)
            if eng_name in trk.dma_final:
                for s_, c in enumerate(trk.dma_final[eng_name]):
                    if c:
                        eng.wait_ge(sems[("dma_" + eng_name, s_)], 16 * c)
        return run

    reg = {"pe": block.tensor, "act": block.scalar, "dve": block.vector,
           "pool": block.gpsimd, "sp": block.sync}
    for e in ENGINES:
        if trk.ops[e]:
            reg[e](body(e))


def _band_consts():
    out = np.zeros((4, 5, 128, 128), np.float32)
    n = S
    for gi, w in enumerate(POOL_WINDOWS):
        def blk(dst_tile, src_tile):
            m = np.zeros((128, 128), np.float32)
            for tl in range(128):
                t = dst_tile * 128 + tl
                lo = min(max(t - w // 2, 0), n)
                hi = min(max(t + w // 2, 0), n)
                cnt = float(hi - lo)
                for s_ in range(lo, hi):
                    sl = s_ - src_tile * 128
                    if 0 <= sl < 128:
                        m[sl, tl] += 1.0 / cnt
                if src_tile == dst_tile:
                    m[tl, tl] -= 1.0
            return m
        out[gi, 0] = blk(5, 5)
        out[gi, 1] = blk(0, 0)
        out[gi, 2] = blk(NLT - 1, NLT - 1)
        out[gi, 3] = blk(5, 4)
        out[gi, 4] = blk(5, 6)
    return np.ascontiguousarray(out.reshape(20, 128, 128).transpose(1, 0, 2)).astype(ml_dtypes.bfloat16)


class _Stop(Exception):
    pass


def build_program(NB, stop_after=None, skip=()):
    nc = bass.Bass("TRN2", target_bir_lowering=False)
    NV = NB + 1

    def din(name, shape, dt=F32):
        return nc.dram_tensor(name, list(shape), dt, kind="ExternalInput").ap()

    x_d = din("x", [NB, S, D])
    ctx_d = din("ctx", [NB, CTXL, D])
    c_d = din("c", [NB, D])
    cctx_d = din("c_ctx", [D])
    wmod_d = din("w_mod", [D, 6 * D])
    bmod_d = din("b_mod", [6 * D])
    gpm_d = din("g_pre_mix", [D])
    gpo_d = din("g_post_mix", [D])
    gpl_d = din("g_pre_mlp", [D])
    gpol_d = din("g_post_mlp", [D])
    win_d = din("w_in", [D, INW])
    bgate_d = din("b_gate", [2 * D])
    gq_d = din("g_q", [HD])
    gk_d = din("g_k", [HD])
    wau_d = din("w_attn_up", [D, D])
    wgrp_d = din("w_pool_grp", [4 * 128, 128])
    psc_d = din("pool_scale", [512])
    wpu_d = din("w_pool_up", [512, D])
    wo_d = din("w_out", [D, D])
    w1_d = din("w_ff1", [D, DFF])
    w2_d = din("w_ff2", [DFF, D])
    ident_d = din("k_ident", [128, 128], BF16)
    bands_d = din("k_bands", [128, 20, 128], BF16)
    out_d = nc.dram_tensor("out", [NB, S, D], F32, kind="ExternalOutput").ap()

    def dsc(name, shape, dt=BF16):
        return nc.dram_tensor(name, list(shape), dt).ap()

    win_b = dsc("win_b", [D, INW])
    wau_b = dsc("wau_b", [D, D])
    wpu_b = dsc("wpu_b", [512, D])
    wgrp_b = dsc("wgrp_b", [512, 128])
    wo_b = dsc("wo_b", [D, D])
    w1_b = dsc("w1_b", [D, DFF])
    w2_b = dsc("w2_b", [DFF, D])
    modsc = dsc("modsc", [NV, 6, D], F32)

    T = Tracker()

    def A(eng, meth, *args, r=(), w=(), **kw):
        T.add(eng, lambda e: getattr(e, meth)(*args, **kw), reads=r, writes=w)

    def DMA(out, in_, r=(), w=(), eng="sp", **kw):
        T.add(eng, lambda e: e.dma_start(out=out, in_=in_, **kw), reads=r, writes=w, dma=True)

    with ExitStack() as st:
        def sb(name, shape, dt):
            return st.enter_context(nc.sbuf_tensor(name, list(shape), dt))

        ident = sb("ident", [128, 128], BF16)
        identf = sb("identf", [128, 128], F32)
        onesb = sb("onesb", [128, 128], BF16)
        bands = sb("bands", [128, 20, 128], BF16)
        cosT = sb("cosT", [128, NLT, 64], F32)
        sinT = sb("sinT", [128, NLT, 64], F32)
        gqb = sb("gqb", [128, 128], F32)
        gkb = sb("gkb", [128, 128], F32)
        bgc = sb("bgc", [128, 16], F32)
        psc = sb("psc", [128, 4], F32)
        epst = sb("epst", [128, 1], F32)
        wgrp = sb("wgrp", [128, 4, 128], BF16)
        AD = sb("AD", [128, 6, D], F32)
        KT = sb("KT", [128, NKV, NKEY], BF16)
        VV = sb("VV", [128, NKT, 256], BF16)
        UU = sb("UU", [128, NLT, 512], BF16)
        XT = sb("XT", [128, 2, D], F32)
        HT = sb("HT", [128, 8, 512], BF16)
        RR = sb("RR", [128, 32, 512], BF16)
        PT = sb("PT", [128, 2, 1024], BF16)
        PO = sb("PO", [128, 4, 512], BF16)
        ZZ = sb("ZZ", [128, 8, 512], BF16)
        X1 = sb("X1", [128, 4, D], F32)
        TF = sb("TF", [128, 2, D], F32)
        HB = sb("HB", [128, 2, D], BF16)
        QN = sb("QN", [128, D], F32)
        QR = sb("QR", [128, D], BF16)
        RL = sb("RL", [128, 2, 512], BF16)
        RD = sb("RD", [128, 512], F32)
        JK = sb("JK", [128, D], BF16)
        WS = sb("WS", [128, 3, 4096], BF16)
        STT = sb("STT", [128, 256], F32)
        SMALL = sb("SMALL", [128, 64], F32)
        SMALLI = sb("SMALLI", [128, 64], I32)
        SMALLR = sb("SMALLR", [16, 256], F32)

        ps01 = st.enter_context(nc.psum_tensor("ps01", [128, 1024], F32))
        ps23 = st.enter_context(nc.psum_tensor("ps23", [128, 1024], F32))
        ps4 = st.enter_context(nc.psum_tensor("ps4", [128, 512], F32))
        ps5 = st.enter_context(nc.psum_tensor("ps5", [128, 512], F32))
        ps6 = st.enter_context(nc.psum_tensor("ps6", [128, 512], F32))
        ps7 = st.enter_context(nc.psum_tensor("ps7", [128, 512], F32))

        def bank(i):
            if i == 0:
                return ps01[:, 0:512]
            if i == 1:
                return ps01[:, 512:1024]
            if i == 2:
                return ps23[:, 0:512]
            if i == 3:
                return ps23[:, 512:1024]
            return (ps4, ps5, ps6, ps7)[i - 4][:]

        def bank2(i):
            return (ps01, ps23)[i][:]

        def pb(i):
            return ("pb", i)

        QTv = RR[:, 0:8, :]
        AOv = RR[:, 8:16, :]
        GTv = RR[:, 16:32, :]
        YAv = ZZ[:].rearrange("p a b -> p (a b)").bitcast(F32).rearrange("p (i c) -> p i c", c=512)

        def ya_keys(i):
            return [("Z", 2 * i + kk, jj) for kk in range(2) for jj in range(4)]

        PDv = PT[:].rearrange("p a b -> p (a b)").rearrange("p (g t) -> p g t", t=512)
        PTK = [("PT", 0), ("PT", 1)]

        stt_ctr = [0]

        def stat(n):
            if stt_ctr[0] + n > 256:
                stt_ctr[0] = 0
            c0 = stt_ctr[0]
            stt_ctr[0] += n
            return STT[:, c0:c0 + n], [("STT", c) for c in range(c0, c0 + n)]

        X1K = [("X1", i) for i in range(4)]

        def chk(n):
            if stop_after is not None and stop_after == n:
                raise _Stop()


        try:
            DMA(ident[:], ident_d[:, :], w=["ident"])
            DMA(bands[:], bands_d[:, :, :], w=["bands"])
            DMA(gqb[:], gq_d.partition_broadcast(128), w=["gqb"])
            DMA(gkb[:], gk_d.partition_broadcast(128), w=["gkb"])
            DMA(SMALLR[0:16, 0:128], bgate_d.rearrange("(c p) -> c p", p=128), w=["smallr_a"])
            DMA(SMALLR[0:4, 128:256], psc_d.rearrange("(c p) -> c p", p=128), w=["smallr_b"])
            A("pool", "memset", onesb[:], 1.0, w=["onesb"])
            A("pool", "memset", epst[:], EPS, w=["epst"])
            A("dve", "tensor_copy", out=identf[:], in_=ident[:], r=["ident"], w=["identf"])
            A("pe", "transpose", out=bank(4)[:, 0:16], in_=SMALLR[0:16, 0:128], identity=identf[0:16, 0:16],
              r=["smallr_a", "identf"], w=[pb(4)])
            A("pe", "transpose", out=bank(4)[:, 16:20], in_=SMALLR[0:4, 128:256], identity=identf[0:4, 0:4],
              r=["smallr_b", "identf"], w=[pb(4)])
            A("dve", "tensor_copy", out=bgc[:], in_=bank(4)[:, 0:16], r=[pb(4)], w=["bgc"])
            A("dve", "tensor_copy", out=psc[:], in_=bank(4)[:, 16:20], r=[pb(4)], w=["psc"])

            pidx = SMALLI[:, 0:1]
            fidx = SMALLI[:, 1:33]
            ridx = SMALLI[:, 33:49]
            pf = SMALL[:, 0:1]
            hi_ = SMALL[:, 1:2]
            colp = SMALL[:, 2:3]
            freq = SMALL[:, 3:35]
            rowp = SMALL[:, 35:51]
            A("pool", "iota", pidx, pattern=[[0, 1]], base=0, channel_multiplier=1, w=["pidx"])
            A("pool", "iota", fidx, pattern=[[1, 32]], base=0, channel_multiplier=0, w=["fidx"])
            A("pool", "iota", ridx, pattern=[[2, 16]], base=0, channel_multiplier=0, w=["ridx"])
            A("dve", "tensor_copy", out=pf, in_=pidx, r=["pidx"], w=["pf"])
            A("dve", "tensor_single_scalar", out=hi_, in_=pf, scalar=64.0, op=ALU.is_ge, r=["pf"], w=["hi"])
            A("dve", "scalar_tensor_tensor", out=colp, in0=hi_, scalar=-64.0, in1=pf, op0=ALU.mult, op1=ALU.add,
              r=["hi", "pf"], w=["colp"])
            A("dve", "tensor_copy", out=freq, in_=fidx, r=["fidx"], w=["freq"])
            A("act", "activation", out=freq, in_=freq, func=AF.Exp, scale=-math.log(10000.0) / 32.0, r=["freq"], w=["freq"])
            A("dve", "tensor_copy", out=rowp, in_=ridx, r=["ridx"], w=["rowp"])
            A("dve", "tensor_scalar", out=rowp, in0=rowp, scalar1=hi_, scalar2=None, op0=ALU.add, r=["rowp", "hi"], w=["rowp"])
            ANG = X1[:, 0, :].rearrange("p (j f) -> p j f", f=64)
            ANGF = X1[:, 0, :]
            KKI = X1[:, 1, :].bitcast(I32)
            KKF = X1[:, 2, :]
            RRD = X1[:, 3, :]
            A("dve", "tensor_tensor", out=ANG[:, :, 0:32], in0=rowp.unsqueeze(2).broadcast_to([128, 16, 32]),
              in1=freq.unsqueeze(1).broadcast_to([128, 16, 32]), op=ALU.mult, r=["rowp", "freq"], w=[("X1", 0)])
            A("dve", "tensor_scalar", out=ANG[:, :, 32:64], in0=freq.unsqueeze(1).broadcast_to([128, 16, 32]),
              scalar1=colp, scalar2=None, op0=ALU.mult, r=["colp", "freq", ("X1", 0)], w=[("X1", 0)])
            for which, shift, dstT in (("sinT", 0.0, sinT), ("cosT", math.pi / 2.0, cosT)):
                A("dve", "tensor_scalar", out=KKI, in0=ANGF, scalar1=shift, scalar2=1.0 / (2 * math.pi),
                  op0=ALU.add, op1=ALU.mult, r=[("X1", 0)], w=[("X1", 1)])
                A("dve", "tensor_copy", out=KKF, in_=KKI, r=[("X1", 1)], w=[("X1", 2)])
                A("dve", "scalar_tensor_tensor", out=RRD, in0=KKF, scalar=-2.0 * math.pi, in1=ANGF,
                  op0=ALU.mult, op1=ALU.add, r=[("X1", 2), ("X1", 0)], w=[("X1", 3)])
                A("dve", "tensor_scalar", out=RRD, in0=RRD, scalar1=shift, scalar2=-math.pi, op0=ALU.add, op1=ALU.max,
                  r=[("X1", 3)], w=[("X1", 3)])
                A("dve", "tensor_scalar", out=RRD, in0=RRD, scalar1=math.pi, scalar2=None, op0=ALU.min,
                  r=[("X1", 3)], w=[("X1", 3)])
                A("act", "activation", out=dstT[:].rearrange("p j f -> p (j f)"), in_=RRD, func=AF.Sin,
                  r=[("X1", 3)], w=[which])

            chk(0)
            def cast(dst, src, rows, name):
                for rb in range(rows // 128):
                    DMA(dst[rb * 128:(rb + 1) * 128, :], src[rb * 128:(rb + 1) * 128, :], w=[("wsc", name, rb)], eng="pool")

            cast(win_b, win_d, D, "win")
            cast(wgrp_b, wgrp_d, 512, "wgrp")
            cast(wau_b, wau_d, D, "wau")
            cast(wpu_b, wpu_d, 512, "wpu")
            cast(wo_b, wo_d, D, "wo")
            cast(w1_b, w1_d, D, "w1")
            cast(w2_b, w2_d, DFF, "w2")
            DMA(wgrp[:], wgrp_b.rearrange("(g c) d -> c g d", c=128), r=[("wsc", "wgrp", rb) for rb in range(4)], w=["wgrp"])

            chk(1)
            crow = TF[0:NV, 0, :]
            srow = TF[0:NV, 1, :]
            DMA(TF[0:NB, 0, :], c_d[:, :], w=[("TF", 0)])
            DMA(TF[NB:NV, 0, :], cctx_d.partition_broadcast(1), w=[("TF", 0)])
            A("act", "activation", out=srow, in_=crow, func=AF.Silu, r=[("TF", 0)], w=[("TF", 1)])
            scT = QN[:, 0:8 * NV].rearrange("p (k v) -> p k v", v=NV)
            for k in range(8):
                A("pe", "transpose", out=bank(4)[:, k * NV:(k + 1) * NV], in_=srow[:, k * 128:(k + 1) * 128],
                  identity=identf[0:NV, 0:NV], r=[("TF", 1), "identf"], w=[pb(4)])
            A("dve", "tensor_copy", out=QN[:, 0:8 * NV], in_=bank(4)[:, 0:8 * NV], r=[pb(4)], w=["QN"])
            WMv = [RR[:, 0:16, :].rearrange("p a b -> p (a b)").bitcast(F32).rearrange("p (k n) -> p k n", n=512),
                   RR[:, 16:32, :].rearrange("p a b -> p (a b)").bitcast(F32).rearrange("p (k n) -> p k n", n=512)]
            WMk = [[("R", f) for f in range(0, 16)], [("R", f) for f in range(16, 32)]]
            gvecs = {1: gpm_d, 2: gpo_d, 4: gpl_d, 5: gpol_d}
            wmod_v = wmod_d.rearrange("(k p) n -> p k n", p=128)
            for n in range(12):
                sl = n % 2
                j, half = n // 2, n % 2
                DMA(WMv[sl], wmod_v[:, :, n * 512:(n + 1) * 512], w=WMk[sl])
                brow = HB[0:NV, sl, :].bitcast(F32)
                mrow = XT[0:NV, sl, 0:512]
                g_t = XT[0:NV, sl, 512:1024]
                DMA(brow, bmod_d[n * 512:(n + 1) * 512].partition_broadcast(NV), w=[("HBk", sl)])
                if j in gvecs:
                    DMA(g_t, gvecs[j][half * 512:(half + 1) * 512].partition_broadcast(NV), w=[("XT", sl)])
                for k in range(8):
                    A("pe", "matmul", bank(5)[0:NV, :], scT[:, k, :], WMv[sl][:, k, :], start=(k == 0), stop=(k == 7),
                      r=["QN"] + WMk[sl], w=[pb(5)])
                A("dve", "tensor_tensor", out=mrow, in0=bank(5)[0:NV, :], in1=brow, op=ALU.add,
                  r=[pb(5), ("HBk", sl)], w=[("XT", sl)])
                if j in (1, 4):
                    A("dve", "scalar_tensor_tensor", out=mrow, in0=mrow, scalar=1.0, in1=g_t, op0=ALU.add, op1=ALU.mult,
                      r=[("XT", sl)], w=[("XT", sl)])
                elif j in (2, 5):
                    A("dve", "tensor_tensor", out=mrow, in0=mrow, in1=g_t, op=ALU.mult, r=[("XT", sl)], w=[("XT", sl)])
                DMA(modsc[:, j, half * 512:(half + 1) * 512], mrow, r=[("XT", sl)], w=[("modsc", n)])
            modsc_keys = [("modsc", n) for n in range(12)]

            chk(2)
            ws_ctr = [0]

            def wload(src_ap, ncols, rkeys, slot=None):
                if slot is None:
                    slot = ws_ctr[0] % 3
                    ws_ctr[0] += 1
                view = WS[:, slot, :].rearrange("p (k n) -> p k n", n=ncols)
                DMA(view, src_ap, r=rkeys, w=[("WS", slot)])
                return view, ("WS", slot)

            def wkeys(name, rbs):
                return [("wsc", name, rb) for rb in rbs]

            win_v = win_b.rearrange("(k p) n -> p k n", p=128)
            wau_v = wau_b.rearrange("(k p) n -> p k n", p=128)
            wpu_v = wpu_b.rearrange("(k p) n -> p k n", p=128)
            wo_v = wo_b.rearrange("(k p) n -> p k n", p=128)
            w1_v = w1_b.rearrange("(k p) n -> p k n", p=128)
            w2_v = w2_b.rearrange("(k p) n -> p k n", p=128)

            def rstd_from_ss(ss, ssk, dim):
                sd, sdk = stat(ss.shape[1])
                rs, rsk = stat(ss.shape[1])
                A("act", "activation", out=sd, in_=ss, func=AF.Sqrt, bias=epst[:, 0:1], scale=1.0 / dim,
                  r=ssk + ["epst"], w=sdk)
                A("dve", "reciprocal", out=rs, in_=sd, r=sdk, w=rsk)
                return rs, rsk

            def norm_part1(src, src_keys, Aj, Bj, sl):
                ss, ssk = stat(1)
                A("act", "activation", out=JK[:], in_=src, func=AF.Square, accum_out=ss, r=src_keys, w=ssk + ["JK"])
                rs, rsk = rstd_from_ss(ss, ssk, D)
                A("dve", "scalar_tensor_tensor", out=TF[:, sl, :], in0=src, scalar=rs, in1=AD[:, Aj, :],
                  op0=ALU.mult, op1=ALU.mult, r=src_keys + rsk + [("AD", Aj)], w=[("TF", sl)])
                A("pool" if sl == 0 else "dve", "tensor_tensor", out=HB[:, sl, :], in0=TF[:, sl, :], in1=AD[:, Bj, :],
                  op=ALU.add, r=[("TF", sl), ("AD", Bj)], w=[("HBk", sl)])

            def norm_part2(sl, i):
                tpv = bank(6).bitcast(BF16)
                for k in range(8):
                    A("pe", "transpose", out=tpv[:, k * 128:(k + 1) * 128], in_=HB[:, sl, k * 128:(k + 1) * 128],
                      identity=ident[:], r=[("HBk", sl), "ident"], w=[pb(6)])
                A("act", "activation", out=HT[:, :, i * 128:(i + 1) * 128], in_=tpv.rearrange("p (k t) -> p k t", t=128),
                  func=AF.Copy, r=[pb(6)], w=[("HT", k, i) for k in range(8)])

            def qk_norm_rope(src_ap, src_keys, nh, gb, gbk, rope_j, dst_ap, dst_keys, defer_b=False, small=False):
                W = nh * 128
                qn = QN[:, 0:W]
                qn3 = qn.rearrange("p (h d) -> p h d", d=128)
                if small:
                    t1 = RD[:, 0:W]
                    t2 = RD[:, 256:256 + W]
                    k1, k2 = "RD0", "RD1"
                else:
                    t1 = TF[:, 0, 0:W]
                    t2 = TF[:, 1, 0:W]
                    k1, k2 = ("TF", 0), ("TF", 1)
                t13 = t1.rearrange("p (h d) -> p h d", d=128)
                t23 = t2.rearrange("p (h d) -> p h d", d=128)
                qr = QR[:, 0:W]
                qr3 = qr.rearrange("p (h d) -> p h d", d=128)
                ssq, ssqk = stat(nh)
                A("act", "activation", out=qn, in_=src_ap, func=AF.Copy, r=src_keys, w=["QN"])
                A("dve", "tensor_tensor", out=t1, in0=qn, in1=qn, op=ALU.mult, r=["QN"], w=[k1])
                A("dve", "tensor_reduce", out=ssq, in_=t13, axis=AX.X, op=ALU.add, r=[k1], w=ssqk)
                rsq, rsqk = rstd_from_ss(ssq, ssqk, HD)
                A("dve", "tensor_tensor", out=qn3, in0=qn3, in1=rsq.unsqueeze(2).broadcast_to([128, nh, 128]), op=ALU.mult,
                  r=["QN"] + rsqk, w=["QN"])
                gbb = gb[:].unsqueeze(1).broadcast_to([128, nh, 128])
                if rope_j is None:
                    A("pool", "tensor_tensor", out=qr3, in0=qn3, in1=gbb, op=ALU.mult, r=["QN", gbk], w=["QR"])
                else:
                    A("dve", "tensor_tensor", out=qn3, in0=qn3, in1=gbb, op=ALU.mult, r=["QN", gbk], w=["QN"])
                    for rc in range(2):
                        o = rc * 64
                        cosb = cosT[:, rope_j, rc * 32:(rc + 1) * 32]
                        sinb = sinT[:, rope_j, rc * 32:(rc + 1) * 32]
                        qv = qn3[:, :, o:o + 64].rearrange("p h (a f) -> p h a f", a=2)
                        t1v = t13[:, :, o:o + 64].rearrange("p h (a f) -> p h a f", a=2)
                        A("dve", "tensor_tensor", out=t1v, in0=qv,
                          in1=cosb.unsqueeze(1).unsqueeze(2).broadcast_to([128, nh, 2, 32]), op=ALU.mult,
                          r=["QN", "cosT"], w=[k1])
                        sb_ = sinb.unsqueeze(1).broadcast_to([128, nh, 32])
                        A("pool", "tensor_tensor", out=t23[:, :, o:o + 32], in0=qn3[:, :, o + 32:o + 64], in1=sb_, op=ALU.mult,
                          r=["QN", "sinT"], w=[k2])
                        A("pool", "tensor_tensor", out=t23[:, :, o + 32:o + 64], in0=qn3[:, :, o:o + 32], in1=sb_, op=ALU.mult,
                          r=["QN", "sinT"], w=[k2])
                    for half, op_, eng in ((0, ALU.subtract, "dve"), (1, ALU.add, "dve")):
                        def hv(t3):
                            return t3.rearrange("p h (r a f) -> p h r a f", r=2, a=2)[:, :, :, half, :]
                        A(eng, "tensor_tensor", out=hv(qr3), in0=hv(t13), in1=hv(t23), op=op_,
                          r=[k1, k2], w=["QR"])
                def part_b():
                    tpv = bank(7).bitcast(BF16)
                    for hh in range(nh):
                        A("pe", "transpose", out=tpv[:, hh * 128:(hh + 1) * 128], in_=qr[:, hh * 128:(hh + 1) * 128],
                          identity=ident[:], r=["QR", "ident"], w=[pb(7)])
                    A("dve", "tensor_copy", out=dst_ap, in_=tpv[:, 0:W].rearrange("p (h t) -> p h t", t=128),
                      r=[pb(7)], w=dst_keys)
                if defer_b:
                    return part_b
                part_b()
                return None

            for b in range(NB):
                DMA(AD[:, 0:2, :], modsc[NB, 0:2, :].partition_broadcast(128), r=modsc_keys, w=[("AD", 0), ("AD", 1)])
                wkv, wkv_k = wload(win_v[:, :, 1024:1536], 512, wkeys("win", range(8)))
                wu, wu_k = wload(win_v[:, :, 1536:2048], 512, wkeys("win", range(8)))

                def a_load(t):
                    sl = t % 2
                    if t < NCT:
                        src = ctx_d[b, t * 128:(t + 1) * 128, :]
                    else:
                        src = x_d[b, (t - NCT) * 128:(t - NCT + 1) * 128, :]
                    DMA(XT[:, sl, :], src, w=[("XT", sl)])

                def a_n1(t):
                    norm_part1(XT[:, t % 2, :], [("XT", t % 2)], 1, 0, t % 2)

                def a_n2(t):
                    norm_part2(t % 2, t % 4)

                def a_proj(t):
                    i = t % 4
                    kvb = t % 2
                    ub = 2 + t % 2
                    for k in range(8):
                        A("pe", "matmul", bank(kvb), HT[:, k, i * 128:(i + 1) * 128], wkv[:, k, :],
                          start=(k == 0), stop=(k == 7), r=[("HT", k, i), wkv_k], w=[pb(kvb)])
                    if t >= NCT:
                        for k in range(8):
                            A("pe", "matmul", bank(ub), HT[:, k, i * 128:(i + 1) * 128], wu[:, k, :],
                              start=(k == 0), stop=(k == 7), r=[("HT", k, i), wu_k], w=[pb(ub)])

                def a_post(t):
                    kvb = t % 2
                    ub = 2 + t % 2
                    A("act", "activation", out=VV[:, t, :], in_=bank(kvb)[:, 256:512], func=AF.Copy, r=[pb(kvb)], w=[("V", t)])
                    if t >= NCT:
                        A("act", "activation", out=UU[:, t - NCT, :], in_=bank(ub), func=AF.Copy, r=[pb(ub)], w=[("U", t - NCT)])
                    qk_norm_rope(bank(kvb)[:, 0:256], [pb(kvb)], 2, gkb, "gkb", (t - NCT) if t >= NCT else None,
                                 KT[:, :, t * 128:(t + 1) * 128], [("KT", t)], small=True)

                a_load(0)
                a_load(1)
                a_n1(0)
                a_n2(0)
                for t in range(NKT):
                    if t + 1 < NKT:
                        if t + 1 == NCT:
                            DMA(AD[:], modsc[b, :, :].partition_broadcast(128), r=modsc_keys, w=[("AD", j) for j in range(6)])
                        a_n1(t + 1)
                    if t + 2 < NKT:
                        a_load(t + 2)
                    a_proj(t)
                    if t + 1 < NKT:
                        a_n2(t + 1)
                    a_post(t)

                chk(3)
                for s in range(4):
                    def xsrc(i):
                        j = 4 * s + i
                        return x_d[b, j * 128:(j + 1) * 128, :]

                    def b_load(i):
                        DMA(XT[:, i % 2, :], xsrc(i), w=[("XT", i % 2)])

                    b_load(0)
                    b_load(1)
                    wq0, wq0_k = wload(win_v[:, :, 0:512], 512, wkeys("win", range(8)))
                    wq1, wq1_k = wload(win_v[:, :, 512:1024], 512, wkeys("win", range(8)))
                    for i in range(4):
                        norm_part1(XT[:, i % 2, :], [("XT", i % 2)], 1, 0, i % 2)
                        if i + 2 < 4:
                            b_load(i + 2)
                        if i >= 1:
                            norm_part2((i - 1) % 2, i - 1)
                    norm_part2(3 % 2, 3)

                    chk(4)
                    def q_proj(i):
                        pr = i % 2
                        for c_, (wq, wqk) in enumerate(((wq0, wq0_k), (wq1, wq1_k))):
                            for k in range(8):
                                A("pe", "matmul", bank(2 * pr + c_), HT[:, k, i * 128:(i + 1) * 128], wq[:, k, :],
                                  start=(k == 0), stop=(k == 7), r=[("HT", k, i), wqk], w=[pb(2 * pr + c_)])

                    def q_rope(i):
                        pr = i % 2
                        return qk_norm_rope(bank2(pr), [pb(2 * pr), pb(2 * pr + 1)], 8, gqb, "gqb", 4 * s + i,
                                            QTv[:, :, i * 128:(i + 1) * 128], [("R", h) for h in range(8)], defer_b=True)

                    gslot = [sl_ for sl_ in range(3) if ("WS", sl_) not in (wq0_k, wq1_k)][0]

                    def gates(c_):
                        wg, wg_k = wload(win_v[:, :, 2048 + c_ * 512:2048 + (c_ + 1) * 512], 512, wkeys("win", range(8)),
                                         slot=gslot)
                        for mm in range(4):
                            m = c_ * 4 + mm
                            gb_ = 4 + (m % 2)
                            for k in range(8):
                                A("pe", "matmul", bank(gb_), wg[:, k, mm * 128:(mm + 1) * 128], HT[:, k, :],
                                  start=(k == 0), stop=(k == 7), r=[("HT", k, ii) for ii in range(4)] + [wg_k], w=[pb(gb_)])
                            A("act", "activation", out=GTv[:, m, :], in_=bank(gb_), func=AF.Sigmoid, bias=bgc[:, m:m + 1],
                              scale=1.0, r=[pb(gb_), "bgc"], w=[("R", 16 + m)])

                    def pool_branch():
                        for i in range(4):
                            j = 4 * s + i
                            pbk = [0, 1, 4, 5][i]
                            for g in range(4):
                                srcs = [(j - 1, 3), (j, 1 if j == 0 else (2 if j == NLT - 1 else 0)), (j + 1, 4)]
                                srcs = [(sj, v) for sj, v in srcs if 0 <= sj < NLT]
                                for n_, (sj, v) in enumerate(srcs):
                                    A("pe", "matmul", bank(pbk)[:, g * 128:(g + 1) * 128], UU[:, sj, g * 128:(g + 1) * 128],
                                      bands[:, g * 5 + v, :], start=(n_ == 0), stop=(n_ == len(srcs) - 1),
                                      r=[("U", sj), "bands"], w=[pb(pbk)])
                            A("act", "activation", out=PDv[:, :, i * 128:(i + 1) * 128],
                              in_=bank(pbk).rearrange("p (g t) -> p g t", t=128), func=AF.Copy, r=[pb(pbk)], w=PTK)
                        for g in range(4):
                            gb_ = 4 + g % 2
                            A("pe", "matmul", bank(gb_), wgrp[:, g, :], PDv[:, g, :], start=True, stop=True,
                              r=["wgrp"] + PTK, w=[pb(gb_)])
                            A("act", "activation", out=PO[:, g, :], in_=bank(gb_), func=AF.Copy, scale=psc[:, g:g + 1],
                              r=[pb(gb_), "psc"], w=[("PO", g)])

                    q_proj(0)
                    b0 = q_rope(0)
                    q_proj(1)
                    gates(0)
                    b0()
                    b1 = q_rope(1)
                    q_proj(2)
                    gates(1)
                    b1()
                    b2 = q_rope(2)
                    q_proj(3)
                    gates(2)
                    b2()
                    b3 = q_rope(3)
                    gates(3)
                    pool_branch()
                    b3()
                    chk(4)
                    chk(5)
                    chk(6)
                    steps = [(h, pr) for h in range(NQH) for pr in range(NKT // 2)]
                    sc = 1.0 / math.sqrt(HD)

                    def qk(idx):
                        h, pr = steps[idx]
                        g = h // 4
                        slot = idx % 2
                        for u in range(2):
                            kt = 2 * pr + u
                            A("pe", "matmul", bank(2 * slot + u), KT[:, g, kt * 128:(kt + 1) * 128], QTv[:, h, :],
                              start=True, stop=True, r=[("KT", kt), ("R", h)], w=[pb(2 * slot + u)])

                    def ex(idx):
                        slot = idx % 2
                        A("act", "activation", out=PT[:, slot, :], in_=bank2(slot), func=AF.Exp, scale=sc,
                          r=[pb(2 * slot), pb(2 * slot + 1)], w=[("PT", slot)])

                    def pv(idx):
                        h, pr = steps[idx]
                        g = h // 4
                        slot = idx % 2
                        ob = 4 + h % 2
                        db = 6 + h % 2
                        for u in range(2):
                            kt = 2 * pr + u
                            first = (pr == 0 and u == 0)
                            last = (pr == NKT // 2 - 1 and u == 1)
                            A("pe", "matmul", bank(ob), VV[:, kt, g * 128:(g + 1) * 128], PT[:, slot, u * 512:(u + 1) * 512],
                              start=first, stop=last, r=[("V", kt), ("PT", slot)], w=[pb(ob)])
                            A("pe", "matmul", bank(db), onesb[:], PT[:, slot, u * 512:(u + 1) * 512],
                              start=first, stop=last, r=["onesb", ("PT", slot)], w=[pb(db)])
                        if pr == NKT // 2 - 1:
                            A("dve", "reciprocal", out=RD[:], in_=bank(db), r=[pb(db)], w=["RD0", "RD1"])
                            A("dve", "tensor_tensor", out=AOv[:, h, :], in0=bank(ob), in1=RD[:], op=ALU.mult,
                              r=[pb(ob), "RD0", "RD1"], w=[("R", 8 + h)])

                    qk(0)
                    for idx in range(len(steps)):
                        if idx + 1 < len(steps):
                            qk(idx + 1)
                        ex(idx)
                        pv(idx)

                    chk(7)
                    chk(8)
                    wpu, wpu_k = wload(wpu_v[:, :, :], 1024, wkeys("wpu", range(4)))
                    wau_c, wau_ck = wload(wau_v[:, :, 0:512], 512, wkeys("wau", range(8)))
                    for m in range(8):
                        c_, mm = m // 4, m % 4
                        if m == 4:
                            wau_c, wau_ck = wload(wau_v[:, :, 512:1024], 512, wkeys("wau", range(8)))
                        par = m % 2
                        yab, ypb = 2 * par, 2 * par + 1
                        for h in range(8):
                            A("pe", "matmul", bank(yab), wau_c[:, h, mm * 128:(mm + 1) * 128], AOv[:, h, :],
                              start=(h == 0), stop=(h == 7), r=[("R", 8 + h), wau_ck], w=[pb(yab)])
                        for g in range(4):
                            A("pe", "matmul", bank(ypb), wpu[:, g, m * 128:(m + 1) * 128], PO[:, g, :],
                              start=(g == 0), stop=(g == 3), r=[("PO", g), wpu_k], w=[pb(ypb)])
                        t1 = TF[:, par, 0:512]
                        t2 = TF[:, par, 512:1024]
                        A("dve", "tensor_tensor", out=t1, in0=bank(yab), in1=GTv[:, m, :], op=ALU.mult,
                          r=[pb(yab), ("R", 16 + m)], w=[("TF", par)])
                        A("dve", "tensor_tensor", out=t2, in0=bank(ypb), in1=GTv[:, 8 + m, :], op=ALU.mult,
                          r=[pb(ypb), ("R", 24 + m)], w=[("TF", par)])
                        A("pool", "tensor_tensor", out=ZZ[:, m, :], in0=t1, in1=t2, op=ALU.add,
                          r=[("TF", par)], w=[("Z", m, ii) for ii in range(4)])

                    wo0, wo0_k = wload(wo_v[:, :, 0:512], 512, wkeys("wo", range(8)))
                    wo1, wo1_k = wload(wo_v[:, :, 512:1024], 512, wkeys("wo", range(8)))
                    for i in range(4):
                        pr = i % 2
                        sl = i % 2
                        DMA(XT[:, sl, :], xsrc(i), w=[("XT", sl)])
                        for c_, (wo_, wok) in enumerate(((wo0, wo0_k), (wo1, wo1_k))):
                            for k in range(8):
                                A("pe", "matmul", bank(2 * pr + c_), ZZ[:, k, i * 128:(i + 1) * 128], wo_[:, k, :],
                                  start=(k == 0), stop=(k == 7), r=[("Z", k, i), wok], w=[pb(2 * pr + c_)])
                        pk = [pb(2 * pr), pb(2 * pr + 1)]
                        ss, ssk = stat(1)
                        A("act", "activation", out=JK[:], in_=bank2(pr), func=AF.Square, accum_out=ss, r=pk, w=ssk + ["JK"])
                        rs, rsk = rstd_from_ss(ss, ssk, D)
                        A("dve", "scalar_tensor_tensor", out=TF[:, sl, :], in0=bank2(pr), scalar=rs, in1=AD[:, 2, :],
                          op0=ALU.mult, op1=ALU.mult, r=pk + [("AD", 2)] + rsk, w=[("TF", sl)])
                        A("dve", "tensor_tensor", out=X1[:, i, :], in0=TF[:, sl, :], in1=XT[:, sl, :], op=ALU.add,
                          r=[("TF", sl), ("XT", sl)], w=[("X1", i)])
                        norm_part1(X1[:, i, :], [("X1", i)], 4, 3, sl)
                        if i >= 1:
                            norm_part2((i - 1) % 2, i - 1)
                    norm_part2(3 % 2, 3)
                    chk(9)
                    aTv = RR
                    for c_ in range(8):
                        w1c, w1c_k = wload(w1_v[:, :, c_ * 512:(c_ + 1) * 512], 512, wkeys("w1", range(8)))
                        for ff in range(4):
                            f = 4 * c_ + ff
                            fb = 4 + f % 2
                            for k in range(8):
                                A("pe", "matmul", bank(fb), w1c[:, k, ff * 128:(ff + 1) * 128], HT[:, k, :],
                                  start=(k == 0), stop=(k == 7), r=[("HT", k, ii) for ii in range(4)] + [w1c_k], w=[pb(fb)])
                            A("act", "activation", out=RL[:, f % 2, :], in_=bank(fb), func=AF.Relu, r=[pb(fb)], w=[("RL", f % 2)])
                            A("pool" if f % 2 == 0 else "dve", "tensor_tensor", out=aTv[:, f, :], in0=RL[:, f % 2, :],
                              in1=RL[:, f % 2, :], op=ALU.mult, r=[("RL", f % 2)], w=[("R", f)])
                    chk(10)
                    ssA, ssAk = stat(4)
                    ssB, ssBk = stat(4)
                    for half in range(2):
                        for pc in range(4):
                            w2p, w2p_k = wload(w2_v[:, 8 * pc:8 * pc + 8, half * 512:(half + 1) * 512], 512,
                                               wkeys("w2", range(8 * pc, 8 * pc + 8)))
                            for i in range(4):
                                for fk in range(8):
                                    f = 8 * pc + fk
                                    A("pe", "matmul", bank(i), aTv[:, f, i * 128:(i + 1) * 128], w2p[:, fk, :],
                                      start=(pc == 0 and fk == 0), stop=(pc == 3 and fk == 7), r=[("R", f), w2p_k], w=[pb(i)])
                        if half == 0:
                            for i in range(4):
                                A("dve", "tensor_copy", out=YAv[:, i, :], in_=bank(i), r=[pb(i)], w=ya_keys(i))
                                A("act", "activation", out=JK[:, 0:512], in_=YAv[:, i, :], func=AF.Square,
                                  accum_out=ssA[:, i:i + 1], r=ya_keys(i), w=[ssAk[i], "JK"])
                            chk(11)
                        else:
                            for i in range(4):
                                j = 4 * s + i
                                sl = i % 2
                                A("act", "activation", out=JK[:, 0:512], in_=bank(i), func=AF.Square, accum_out=ssB[:, i:i + 1],
                                  r=[pb(i)], w=[ssBk[i], "JK"])
                                A("dve", "tensor_tensor", out=ssB[:, i:i + 1], in0=ssB[:, i:i + 1], in1=ssA[:, i:i + 1], op=ALU.add,
                                  r=[ssAk[i], ssBk[i]], w=[ssBk[i]])
                                rs, rsk = rstd_from_ss(ssB[:, i:i + 1], [ssBk[i]], D)
                                A("dve", "scalar_tensor_tensor", out=TF[:, sl, 0:512], in0=YAv[:, i, :], scalar=rs,
                                  in1=AD[:, 5, 0:512], op0=ALU.mult, op1=ALU.mult, r=ya_keys(i) + rsk + [("AD", 5)], w=[("TF", sl)])
                                A("dve", "scalar_tensor_tensor", out=TF[:, sl, 512:1024], in0=bank(i), scalar=rs,
                                  in1=AD[:, 5, 512:1024], op0=ALU.mult, op1=ALU.mult, r=[pb(i), ("AD", 5)] + rsk, w=[("TF", sl)])
                                A("pool", "tensor_tensor", out=X1[:, i, :], in0=X1[:, i, :], in1=TF[:, sl, :], op=ALU.add,
                                  r=[("X1", i), ("TF", sl)], w=[("X1", i)])
                                DMA(out_d[b, j * 128:(j + 1) * 128, :], X1[:, i, :], r=[("X1", i)], w=[("out", b, j)])
                    chk(12)

        except _Stop:
            pass
        emit_program(nc, T, st)
    return nc


_CACHE = {}


def _get_program(NB):
    if NB not in _CACHE:
        _CACHE[NB] = build_program(NB)
    return _CACHE[NB]


def kernel(x, c, ctx, c_ctx, w_mod, b_mod, g_pre_mix, g_post_mix, g_pre_mlp, g_post_mlp,
           w_in, b_gate, g_q, g_k, w_attn_up, w_pool_grp, pool_scale, w_pool_up, w_out,
           w_ff1, w_ff2):
    f = lambda a: np.ascontiguousarray(np.asarray(a, dtype=np.float32))
    x = f(x); c = f(c); ctx = f(ctx)
    B = x.shape[0]
    NB = B // N_CORES
    nc = _get_program(NB)
    shared = {
        "c_ctx": f(c_ctx), "w_mod": f(w_mod)[0], "b_mod": f(b_mod)[0],
        "g_pre_mix": f(g_pre_mix)[0], "g_post_mix": f(g_post_mix)[0],
        "g_pre_mlp": f(g_pre_mlp)[0], "g_post_mlp": f(g_post_mlp)[0],
        "w_in": f(w_in)[0], "b_gate": f(b_gate)[0], "g_q": f(g_q)[0], "g_k": f(g_k)[0],
        "w_attn_up": f(w_attn_up)[0], "w_pool_grp": f(w_pool_grp)[0].reshape(512, 128),
        "pool_scale": f(pool_scale)[0], "w_pool_up": f(w_pool_up)[0], "w_out": f(w_out)[0],
        "w_ff1": f(w_ff1)[0], "w_ff2": f(w_ff2)[0],
        "k_ident": np.eye(128, dtype=np.float32).astype(ml_dtypes.bfloat16),
        "k_bands": _band_consts(),
    }
    in_maps = []
    for i in range(N_CORES):
        m = dict(shared)
        m["x"] = x[i * NB:(i + 1) * NB]
        m["c"] = c[i * NB:(i + 1) * NB]
        m["ctx"] = ctx[i * NB:(i + 1) * NB]
        in_maps.append(m)
    res = run_bass_kernel_spmd(nc, in_maps, core_ids=list(range(N_CORES)))
    out = np.concatenate([np.asarray(r["out"]) for r in res.results], axis=0)
    return out.astype(np.float32)
```

```python
import math
from contextlib import ExitStack

import numpy as np
import ml_dtypes

import concourse.bass as bass
import concourse.mybir as mybir
from concourse.bass_utils import run_bass_kernel_spmd

F32 = mybir.dt.float32
BF16 = mybir.dt.bfloat16
I32 = mybir.dt.int32
AF = mybir.ActivationFunctionType
ALU = mybir.AluOpType
AX = mybir.AxisListType

N_CORES = 8
D = 1024
S = 2048
CTXL = 256
HD = 128
NQH = 8
NKV = 2
INW = 4096
DFF = 4096
NLT = S // 128
NCT = CTXL // 128
NKT = NLT + NCT
NKEY = NKT * 128
EPS = 1e-6
POOL_WINDOWS = (2, 4, 8, 16)

ENGINES = ("pe", "act", "dve", "pool", "sp")
SEM_ROT = 30000
DMA_RING = {"sp": 12, "pool": 8}


class _Op:
    __slots__ = ("eng", "fn", "dma", "deps", "signal", "ticket")

    def __init__(self, eng, fn, dma):
        self.eng = eng
        self.fn = fn
        self.dma = dma
        self.deps = []
        self.signal = False
        self.ticket = None


class Tracker:
    def __init__(self):
        self.ops = {e: [] for e in ENGINES}
        self.last_w = {}
        self.readers = {}
        self.dma_ops = {e: [] for e in DMA_RING}

    def add(self, eng, fn, reads=(), writes=(), dma=False):
        op = _Op(eng, fn, dma)
        deps = {}

        def need(d, kind):
            if d is None or d is op:
                return
            if (not d.dma) and (not dma) and d.eng == eng:
                if eng == "pe":
                    return
            deps[id(d)] = d

        for r in reads:
            need(self.last_w.get(r), "raw")
            if isinstance(r, tuple) and r and r[0] == "pb":
                rd = self.readers.get(r)
                if rd:
                    for k, v in rd.items():
                        if k != "_dma" and k != eng:
                            need(v, "rar")
        for w in writes:
            need(self.last_w.get(w), "waw")
            rd = self.readers.get(w)
            if rd:
                for k, v in rd.items():
                    if k == "_dma":
                        for d in v:
                            need(d, "war")
                    else:
                        need(v, "war")
        if dma:
            lst = self.dma_ops[eng]
            ring = DMA_RING[eng]
            if len(lst) >= ring:
                d = lst[len(lst) - ring]
                deps[id(d)] = d
            lst.append(op)
        op.deps = list(deps.values())
        for d in op.deps:
            d.signal = True
        for r in reads:
            rd = self.readers.setdefault(r, {})
            if dma:
                rd.setdefault("_dma", []).append(op)
            else:
                rd[eng] = op
        for w in writes:
            self.last_w[w] = op
            self.readers[w] = {}
        self.ops[eng].append(op)
        return op

    def finalize(self):
        need = {}
        for e in ENGINES:
            c = 0
            for op in self.ops[e]:
                if op.dma:
                    continue
                if op.signal:
                    c += 1
                    op.ticket = (e, (c - 1) // SEM_ROT, (c - 1) % SEM_ROT + 1)
            need[e] = (c + SEM_ROT - 1) // SEM_ROT
        self.dma_final = {}
        for e, lst in self.dma_ops.items():
            ring = DMA_RING[e]
            cnt = [0] * ring
            for i, op in enumerate(lst):
                s = i % ring
                cnt[s] += 1
                op.ticket = ("dma_" + e, s, 16 * cnt[s])
            self.dma_final[e] = cnt
        return need


def emit_program(nc, trk, stack):
    need = trk.finalize()
    sems = {}
    for e in ENGINES:
        for r in range(need[e]):
            sems[(e, r)] = stack.enter_context(nc.semaphore(f"pg_{e}_{r}"))
    for e, ring in DMA_RING.items():
        for s in range(ring):
            sems[("dma_" + e, s)] = stack.enter_context(nc.semaphore(f"dq_{e}_{s}"))
    block = stack.enter_context(nc.Block())

    def body(eng_name):
        def run(eng):
            waited = {}
            for op in trk.ops[eng_name]:
                for d in op.deps:
                    k, r, v = d.ticket
                    key = (k, r)
                    if waited.get(key, 0) >= v:
                        continue
                    waited[key] = v
                    eng.wait_ge(sems[key], v)
                ins = op.fn(eng)
                k, r, v = op.ticket if op.ticket is not None else (None, None, None)
                if op.dma:
                    ins.then_inc(sems[(k, r)], 16)
                elif op.signal:
                    ins.then_inc(sems[(k, r)], 1)
            if eng_name in trk.dma_final:
                for s_, c in enumerate(trk.dma_final[eng_name]):
                    if c:
                        eng.wait_ge(sems[("dma_" + eng_name, s_)], 16 * c)
        return run

    reg = {"pe": block.tensor, "act": block.scalar, "dve": block.vector,
           "pool": block.gpsimd, "sp": block.sync}
    for e in ENGINES:
        if trk.ops[e]:
            reg[e](body(e))


def _band_consts():
    out = np.zeros((4, 5, 128, 128), np.float32)
    n = S
    for gi, w in enumerate(POOL_WINDOWS):
        def blk(dst_tile, src_tile):
            m = np.zeros((128, 128), np.float32)
            for tl in range(128):
                t = dst_tile * 128 + tl
                lo = min(max(t - w // 2, 0), n)
                hi = min(max(t + w // 2, 0), n)
                cnt = float(hi - lo)
                for s_ in range(lo, hi):
                    sl = s_ - src_tile * 128
                    if 0 <= sl < 128:
                        m[sl, tl] += 1.0 / cnt
                if src_tile == dst_tile:
                    m[tl, tl] -= 1.0
            return m
        out[gi, 0] = blk(5, 5)
        out[gi, 1] = blk(0, 0)
        out[gi, 2] = blk(NLT - 1, NLT - 1)
        out[gi, 3] = blk(5, 4)
        out[gi, 4] = blk(5, 6)
    return np.ascontiguousarray(out.reshape(20, 128, 128).transpose(1, 0, 2)).astype(ml_dtypes.bfloat16)


class _Stop(Exception):
    pass


def build_program(NB, stop_after=None, skip=()):
    nc = bass.Bass("TRN2", target_bir_lowering=False)
    NV = NB + 1

    def din(name, shape, dt=F32):
        return nc.dram_tensor(name, list(shape), dt, kind="ExternalInput").ap()

    x_d = din("x", [NB, S, D])
    ctx_d = din("ctx", [NB, CTXL, D])
    c_d = din("c", [NB, D])
    cctx_d = din("c_ctx", [D])
    wmod_d = din("w_mod", [D, 6 * D])
    bmod_d = din("b_mod", [6 * D])
    gpm_d = din("g_pre_mix", [D])
    gpo_d = din("g_post_mix", [D])
    gpl_d = din("g_pre_mlp", [D])
    gpol_d = din("g_post_mlp", [D])
    win_d = din("w_in", [D, INW])
    bgate_d = din("b_gate", [2 * D])
    gq_d = din("g_q", [HD])
    gk_d = din("g_k", [HD])
    wau_d = din("w_attn_up", [D, D])
    wgrp_d = din("w_pool_grp", [4 * 128, 128])
    psc_d = din("pool_scale", [512])
    wpu_d = din("w_pool_up", [512, D])
    wo_d = din("w_out", [D, D])
    w1_d = din("w_ff1", [D, DFF])
    w2_d = din("w_ff2", [DFF, D])
    ident_d = din("k_ident", [128, 128], BF16)
    bands_d = din("k_bands", [128, 20, 128], BF16)
    out_d = nc.dram_tensor("out", [NB, S, D], F32, kind="ExternalOutput").ap()

    def dsc(name, shape, dt=BF16):
        return nc.dram_tensor(name, list(shape), dt).ap()

    win_b = dsc("win_b", [D, INW])
    wau_b = dsc("wau_b", [D, D])
    wpu_b = dsc("wpu_b", [512, D])
    wgrp_b = dsc("wgrp_b", [512, 128])
    wo_b = dsc("wo_b", [D, D])
    w1_b = dsc("w1_b", [D, DFF])
    w2_b = dsc("w2_b", [DFF, D])
    modsc = dsc("modsc", [NV, 6, D], F32)

    T = Tracker()

    def A(eng, meth, *args, r=(), w=(), **kw):
        T.add(eng, lambda e: getattr(e, meth)(*args, **kw), reads=r, writes=w)

    def DMA(out, in_, r=(), w=(), eng="sp", **kw):
        T.add(eng, lambda e: e.dma_start(out=out, in_=in_, **kw), reads=r, writes=w, dma=True)

    with ExitStack() as st:
        def sb(name, shape, dt):
            return st.enter_context(nc.sbuf_tensor(name, list(shape), dt))

        ident = sb("ident", [128, 128], BF16)
        identf = sb("identf", [128, 128], F32)
        onesb = sb("onesb", [128, 128], BF16)
        bands = sb("bands", [128, 20, 128], BF16)
        cosT = sb("cosT", [128, NLT, 64], F32)
        sinT = sb("sinT", [128, NLT, 64], F32)
        gqb = sb("gqb", [128, 128], F32)
        gkb = sb("gkb", [128, 128], F32)
        bgc = sb("bgc", [128, 16], F32)
        psc = sb("psc", [128, 4], F32)
        epst = sb("epst", [128, 1], F32)
        wgrp = sb("wgrp", [128, 4, 128], BF16)
        AD = sb("AD", [128, 6, D], F32)
        KT = sb("KT", [128, NKV, NKEY], BF16)
        VV = sb("VV", [128, NKT, 256], BF16)
        UU = sb("UU", [128, NLT, 512], BF16)
        XT = sb("XT", [128, 2, D], F32)
        HT = sb("HT", [128, 8, 512], BF16)
        RR = sb("RR", [128, 32, 512], BF16)
        PT = sb("PT", [128, 2, 1024], BF16)
        PO = sb("PO", [128, 4, 512], BF16)
        ZZ = sb("ZZ", [128, 8, 512], BF16)
        X1 = sb("X1", [128, 4, D], F32)
        TF = sb("TF", [128, 2, D], F32)
        HB = sb("HB", [128, 2, D], BF16)
        QN = sb("QN", [128, D], F32)
        QR = sb("QR", [128, D], BF16)
        RL = sb("RL", [128, 2, 512], BF16)
        RD = sb("RD", [128, 512], F32)
        JK = sb("JK", [128, D], BF16)
        WS = sb("WS", [128, 3, 4096], BF16)
        STT = sb("STT", [128, 256], F32)
        SMALL = sb("SMALL", [128, 64], F32)
        SMALLI = sb("SMALLI", [128, 64], I32)
        SMALLR = sb("SMALLR", [16, 256], F32)

        ps01 = st.enter_context(nc.psum_tensor("ps01", [128, 1024], F32))
        ps23 = st.enter_context(nc.psum_tensor("ps23", [128, 1024], F32))
        ps4 = st.enter_context(nc.psum_tensor("ps4", [128, 512], F32))
        ps5 = st.enter_context(nc.psum_tensor("ps5", [128, 512], F32))
        ps6 = st.enter_context(nc.psum_tensor("ps6", [128, 512], F32))
        ps7 = st.enter_context(nc.psum_tensor("ps7", [128, 512], F32))

        def bank(i):
            if i == 0:
                return ps01[:, 0:512]
            if i == 1:
                return ps01[:, 512:1024]
            if i == 2:
                return ps23[:, 0:512]
            if i == 3:
                return ps23[:, 512:1024]
            return (ps4, ps5, ps6, ps7)[i - 4][:]

        def bank2(i):
            return (ps01, ps23)[i][:]

        def pb(i):
            return ("pb", i)

        QTv = RR[:, 0:8, :]
        AOv = RR[:, 8:16, :]
        GTv = RR[:, 16:32, :]
        YAv = ZZ[:].rearrange("p a b -> p (a b)").bitcast(F32).rearrange("p (i c) -> p i c", c=512)

        def ya_keys(i):
            return [("Z", 2 * i + kk, jj) for kk in range(2) for jj in range(4)]

        PDv = PT[:].rearrange("p a b -> p (a b)").rearrange("p (g t) -> p g t", t=512)
        PTK = [("PT", 0), ("PT", 1)]

        stt_ctr = [0]

        def stat(n):
            if stt_ctr[0] + n > 256:
                stt_ctr[0] = 0
            c0 = stt_ctr[0]
            stt_ctr[0] += n
            return STT[:, c0:c0 + n], [("STT", c) for c in range(c0, c0 + n)]

        X1K = [("X1", i) for i in range(4)]

        def chk(n):
            if stop_after is not None and stop_after == n:
                raise _Stop()


        try:
            DMA(ident[:], ident_d[:, :], w=["ident"])
            DMA(bands[:], bands_d[:, :, :], w=["bands"])
            DMA(gqb[:], gq_d.partition_broadcast(128), w=["gqb"])
            DMA(gkb[:], gk_d.partition_broadcast(128), w=["gkb"])
            DMA(SMALLR[0:16, 0:128], bgate_d.rearrange("(c p) -> c p", p=128), w=["smallr_a"])
            DMA(SMALLR[0:4, 128:256], psc_d.rearrange("(c p) -> c p", p=128), w=["smallr_b"])
            A("pool", "memset", onesb[:], 1.0, w=["onesb"])
            A("pool", "memset", epst[:], EPS, w=["epst"])
            A("dve", "tensor_copy", out=identf[:], in_=ident[:], r=["ident"], w=["identf"])
            A("pe", "transpose", out=bank(4)[:, 0:16], in_=SMALLR[0:16, 0:128], identity=identf[0:16, 0:16],
              r=["smallr_a", "identf"], w=[pb(4)])
            A("pe", "transpose", out=bank(4)[:, 16:20], in_=SMALLR[0:4, 128:256], identity=identf[0:4, 0:4],
              r=["smallr_b", "identf"], w=[pb(4)])
            A("dve", "tensor_copy", out=bgc[:], in_=bank(4)[:, 0:16], r=[pb(4)], w=["bgc"])
            A("dve", "tensor_copy", out=psc[:], in_=bank(4)[:, 16:20], r=[pb(4)], w=["psc"])

            pidx = SMALLI[:, 0:1]
            fidx = SMALLI[:, 1:33]
            ridx = SMALLI[:, 33:49]
            pf = SMALL[:, 0:1]
            hi_ = SMALL[:, 1:2]
            colp = SMALL[:, 2:3]
            freq = SMALL[:, 3:35]
            rowp = SMALL[:, 35:51]
            A("pool", "iota", pidx, pattern=[[0, 1]], base=0, channel_multiplier=1, w=["pidx"])
            A("pool", "iota", fidx, pattern=[[1, 32]], base=0, channel_multiplier=0, w=["fidx"])
            A("pool", "iota", ridx, pattern=[[2, 16]], base=0, channel_multiplier=0, w=["ridx"])
            A("dve", "tensor_copy", out=pf, in_=pidx, r=["pidx"], w=["pf"])
            A("dve", "tensor_single_scalar", out=hi_, in_=pf, scalar=64.0, op=ALU.is_ge, r=["pf"], w=["hi"])
            A("dve", "scalar_tensor_tensor", out=colp, in0=hi_, scalar=-64.0, in1=pf, op0=ALU.mult, op1=ALU.add,
              r=["hi", "pf"], w=["colp"])
            A("dve", "tensor_copy", out=freq, in_=fidx, r=["fidx"], w=["freq"])
            A("act", "activation", out=freq, in_=freq, func=AF.Exp, scale=-math.log(10000.0) / 32.0, r=["freq"], w=["freq"])
            A("dve", "tensor_copy", out=rowp, in_=ridx, r=["ridx"], w=["rowp"])
            A("dve", "tensor_scalar", out=rowp, in0=rowp, scalar1=hi_, scalar2=None, op0=ALU.add, r=["rowp", "hi"], w=["rowp"])
            ANG = X1[:, 0, :].rearrange("p (j f) -> p j f", f=64)
            ANGF = X1[:, 0, :]
            KKI = X1[:, 1, :].bitcast(I32)
            KKF = X1[:, 2, :]
            RRD = X1[:, 3, :]
            A("dve", "tensor_tensor", out=ANG[:, :, 0:32], in0=rowp.unsqueeze(2).broadcast_to([128, 16, 32]),
              in1=freq.unsqueeze(1).broadcast_to([128, 16, 32]), op=ALU.mult, r=["rowp", "freq"], w=[("X1", 0)])
            A("dve", "tensor_scalar", out=ANG[:, :, 32:64], in0=freq.unsqueeze(1).broadcast_to([128, 16, 32]),
              scalar1=colp, scalar2=None, op0=ALU.mult, r=["colp", "freq", ("X1", 0)], w=[("X1", 0)])
            for which, shift, dstT in (("sinT", 0.0, sinT), ("cosT", math.pi / 2.0, cosT)):
                A("dve", "tensor_scalar", out=KKI, in0=ANGF, scalar1=shift, scalar2=1.0 / (2 * math.pi),
                  op0=ALU.add, op1=ALU.mult, r=[("X1", 0)], w=[("X1", 1)])
                A("dve", "tensor_copy", out=KKF, in_=KKI, r=[("X1", 1)], w=[("X1", 2)])
                A("dve", "scalar_tensor_tensor", out=RRD, in0=KKF, scalar=-2.0 * math.pi, in1=ANGF,
                  op0=ALU.mult, op1=ALU.add, r=[("X1", 2), ("X1", 0)], w=[("X1", 3)])
                A("dve", "tensor_scalar", out=RRD, in0=RRD, scalar1=shift, scalar2=-math.pi, op0=ALU.add, op1=ALU.max,
                  r=[("X1", 3)], w=[("X1", 3)])
                A("dve", "tensor_scalar", out=RRD, in0=RRD, scalar1=math.pi, scalar2=None, op0=ALU.min,
                  r=[("X1", 3)], w=[("X1", 3)])
                A("act", "activation", out=dstT[:].rearrange("p j f -> p (j f)"), in_=RRD, func=AF.Sin,
                  r=[("X1", 3)], w=[which])

            chk(0)
            def cast(dst, src, rows, name):
                for rb in range(rows // 128):
                    DMA(dst[rb * 128:(rb + 1) * 128, :], src[rb * 128:(rb + 1) * 128, :], w=[("wsc", name, rb)], eng="pool")

            cast(win_b, win_d, D, "win")
            cast(wgrp_b, wgrp_d, 512, "wgrp")
            cast(wau_b, wau_d, D, "wau")
            cast(wpu_b, wpu_d, 512, "wpu")
            cast(wo_b, wo_d, D, "wo")
            cast(w1_b, w1_d, D, "w1")
            cast(w2_b, w2_d, DFF, "w2")
            DMA(wgrp[:], wgrp_b.rearrange("(g c) d -> c g d", c=128), r=[("wsc", "wgrp", rb) for rb in range(4)], w=["wgrp"])

            chk(1)
            crow = TF[0:NV, 0, :]
            srow = TF[0:NV, 1, :]
            DMA(TF[0:NB, 0, :], c_d[:, :], w=[("TF", 0)])
            DMA(TF[NB:NV, 0, :], cctx_d.partition_broadcast(1), w=[("TF", 0)])
            A("act", "activation", out=srow, in_=crow, func=AF.Silu, r=[("TF", 0)], w=[("TF", 1)])
            scT = QN[:, 0:8 * NV].rearrange("p (k v) -> p k v", v=NV)
            for k in range(8):
                A("pe", "transpose", out=bank(4)[:, k * NV:(k + 1) * NV], in_=srow[:, k * 128:(k + 1) * 128],
                  identity=identf[0:NV, 0:NV], r=[("TF", 1), "identf"], w=[pb(4)])
            A("dve", "tensor_copy", out=QN[:, 0:8 * NV], in_=bank(4)[:, 0:8 * NV], r=[pb(4)], w=["QN"])
            WMv = [RR[:, 0:16, :].rearrange("p a b -> p (a b)").bitcast(F32).rearrange("p (k n) -> p k n", n=512),
                   RR[:, 16:32, :].rearrange("p a b -> p (a b)").bitcast(F32).rearrange("p (k n) -> p k n", n=512)]
            WMk = [[("R", f) for f in range(0, 16)], [("R", f) for f in range(16, 32)]]
            gvecs = {1: gpm_d, 2: gpo_d, 4: gpl_d, 5: gpol_d}
            wmod_v = wmod_d.rearrange("(k p) n -> p k n", p=128)
            for n in range(12):
                sl = n % 2
                j, half = n // 2, n % 2
                DMA(WMv[sl], wmod_v[:, :, n * 512:(n + 1) * 512], w=WMk[sl])
                brow = HB[0:NV, sl, :].bitcast(F32)
                mrow = XT[0:NV, sl, 0:512]
                g_t = XT[0:NV, sl, 512:1024]
                DMA(brow, bmod_d[n * 512:(n + 1) * 512].partition_broadcast(NV), w=[("HBk", sl)])
                if j in gvecs:
                    DMA(g_t, gvecs[j][half * 512:(half + 1) * 512].partition_broadcast(NV), w=[("XT", sl)])
                for k in range(8):
                    A("pe", "matmul", bank(5)[0:NV, :], scT[:, k, :], WMv[sl][:, k, :], start=(k == 0), stop=(k == 7),
                      r=["QN"] + WMk[sl], w=[pb(5)])
                A("dve", "tensor_tensor", out=mrow, in0=bank(5)[0:NV, :], in1=brow, op=ALU.add,
                  r=[pb(5), ("HBk", sl)], w=[("XT", sl)])
                if j in (1, 4):
                    A("dve", "scalar_tensor_tensor", out=mrow, in0=mrow, scalar=1.0, in1=g_t, op0=ALU.add, op1=ALU.mult,
                      r=[("XT", sl)], w=[("XT", sl)])
                elif j in (2, 5):
                    A("dve", "tensor_tensor", out=mrow, in0=mrow, in1=g_t, op=ALU.mult, r=[("XT", sl)], w=[("XT", sl)])
                DMA(modsc[:, j, half * 512:(half + 1) * 512], mrow, r=[("XT", sl)], w=[("modsc", n)])
            modsc_keys = [("modsc", n) for n in range(12)]

            chk(2)
            ws_ctr = [0]

            def wload(src_ap, ncols, rkeys, slot=None):
                if slot is None:
                    slot = ws_ctr[0] % 3
                    ws_ctr[0] += 1
                view = WS[:, slot, :].rearrange("p (k n) -> p k n", n=ncols)
                DMA(view, src_ap, r=rkeys, w=[("WS", slot)])
                return view, ("WS", slot)

            def wkeys(name, rbs):
                return [("wsc", name, rb) for rb in rbs]

            win_v = win_b.rearrange("(k p) n -> p k n", p=128)
            wau_v = wau_b.rearrange("(k p) n -> p k n", p=128)
            wpu_v = wpu_b.rearrange("(k p) n -> p k n", p=128)
            wo_v = wo_b.rearrange("(k p) n -> p k n", p=128)
            w1_v = w1_b.rearrange("(k p) n -> p k n", p=128)
            w2_v = w2_b.rearrange("(k p) n -> p k n", p=128)

            def rstd_from_ss(ss, ssk, dim):
                sd, sdk = stat(ss.shape[1])
                rs, rsk = stat(ss.shape[1])
                A("act", "activation", out=sd, in_=ss, func=AF.Sqrt, bias=epst[:, 0:1], scale=1.0 / dim,
                  r=ssk + ["epst"], w=sdk)
                A("dve", "reciprocal", out=rs, in_=sd, r=sdk, w=rsk)
                return rs, rsk

            def norm_part1(src, src_keys, Aj, Bj, sl):
                ss, ssk = stat(1)
                A("act", "activation", out=JK[:], in_=src, func=AF.Square, accum_out=ss, r=src_keys, w=ssk + ["JK"])
                rs, rsk = rstd_from_ss(ss, ssk, D)
                A("dve", "scalar_tensor_tensor", out=TF[:, sl, :], in0=src, scalar=rs, in1=AD[:, Aj, :],
                  op0=ALU.mult, op1=ALU.mult, r=src_keys + rsk + [("AD", Aj)], w=[("TF", sl)])
                A("pool" if sl == 0 else "dve", "tensor_tensor", out=HB[:, sl, :], in0=TF[:, sl, :], in1=AD[:, Bj, :],
                  op=ALU.add, r=[("TF", sl), ("AD", Bj)], w=[("HBk", sl)])

            def norm_part2(sl, i):
                tpv = bank(6).bitcast(BF16)
                for k in range(8):
                    A("pe", "transpose", out=tpv[:, k * 128:(k + 1) * 128], in_=HB[:, sl, k * 128:(k + 1) * 128],
                      identity=ident[:], r=[("HBk", sl), "ident"], w=[pb(6)])
                A("act", "activation", out=HT[:, :, i * 128:(i + 1) * 128], in_=tpv.rearrange("p (k t) -> p k t", t=128),
                  func=AF.Copy, r=[pb(6)], w=[("HT", k, i) for k in range(8)])

            def qk_norm_rope(src_ap, src_keys, nh, gb, gbk, rope_j, dst_ap, dst_keys, defer_b=False, small=False):
                W = nh * 128
                qn = QN[:, 0:W]
                qn3 = qn.rearrange("p (h d) -> p h d", d=128)
                if small:
                    t1 = RD[:, 0:W]
                    t2 = RD[:, 256:256 + W]
                    k1, k2 = "RD0", "RD1"
                else:
                    t1 = TF[:, 0, 0:W]
                    t2 = TF[:, 1, 0:W]
                    k1, k2 = ("TF", 0), ("TF", 1)
                t13 = t1.rearrange("p (h d) -> p h d", d=128)
                t23 = t2.rearrange("p (h d) -> p h d", d=128)
                qr = QR[:, 0:W]
                qr3 = qr.rearrange("p (h d) -> p h d", d=128)
                ssq, ssqk = stat(nh)
                A("act", "activation", out=qn, in_=src_ap, func=AF.Copy, r=src_keys, w=["QN"])
                A("dve", "tensor_tensor", out=t1, in0=qn, in1=qn, op=ALU.mult, r=["QN"], w=[k1])
                A("dve", "tensor_reduce", out=ssq, in_=t13, axis=AX.X, op=ALU.add, r=[k1], w=ssqk)
                rsq, rsqk = rstd_from_ss(ssq, ssqk, HD)
                A("dve", "tensor_tensor", out=qn3, in0=qn3, in1=rsq.unsqueeze(2).broadcast_to([128, nh, 128]), op=ALU.mult,
                  r=["QN"] + rsqk, w=["QN"])
                gbb = gb[:].unsqueeze(1).broadcast_to([128, nh, 128])
                if rope_j is None:
                    A("pool", "tensor_tensor", out=qr3, in0=qn3, in1=gbb, op=ALU.mult, r=["QN", gbk], w=["QR"])
                else:
                    A("dve", "tensor_tensor", out=qn3, in0=qn3, in1=gbb, op=ALU.mult, r=["QN", gbk], w=["QN"])
                    for rc in range(2):
                        o = rc * 64
                        cosb = cosT[:, rope_j, rc * 32:(rc + 1) * 32]
                        sinb = sinT[:, rope_j, rc * 32:(rc + 1) * 32]
                        qv = qn3[:, :, o:o + 64].rearrange("p h (a f) -> p h a f", a=2)
                        t1v = t13[:, :, o:o + 64].rearrange("p h (a f) -> p h a f", a=2)
                        A("dve", "tensor_tensor", out=t1v, in0=qv,
                          in1=cosb.unsqueeze(1).unsqueeze(2).broadcast_to([128, nh, 2, 32]), op=ALU.mult,
                          r=["QN", "cosT"], w=[k1])
                        sb_ = sinb.unsqueeze(1).broadcast_to([128, nh, 32])
                        A("pool", "tensor_tensor", out=t23[:, :, o:o + 32], in0=qn3[:, :, o + 32:o + 64], in1=sb_, op=ALU.mult,
                          r=["QN", "sinT"], w=[k2])
                        A("pool", "tensor_tensor", out=t23[:, :, o + 32:o + 64], in0=qn3[:, :, o:o + 32], in1=sb_, op=ALU.mult,
                          r=["QN", "sinT"], w=[k2])
                    for half, op_, eng in ((0, ALU.subtract, "dve"), (1, ALU.add, "dve")):
                        def hv(t3):
                            return t3.rearrange("p h (r a f) -> p h r a f", r=2, a=2)[:, :, :, half, :]
                        A(eng, "tensor_tensor", out=hv(qr3), in0=hv(t13), in1=hv(t23), op=op_,
                          r=[k1, k2], w=["QR"])
                def part_b():
                    tpv = bank(7).bitcast(BF16)
                    for hh in range(nh):
                        A("pe", "transpose", out=tpv[:, hh * 128:(hh + 1) * 128], in_=qr[:, hh * 128:(hh + 1) * 128],
                          identity=ident[:], r=["QR", "ident"], w=[pb(7)])
                    A("dve", "tensor_copy", out=dst_ap, in_=tpv[:, 0:W].rearrange("p (h t) -> p h t", t=128),
                      r=[pb(7)], w=dst_keys)
                if defer_b:
                    return part_b
                part_b()
                return None

            for b in range(NB):
                DMA(AD[:, 0:2, :], modsc[NB, 0:2, :].partition_broadcast(128), r=modsc_keys, w=[("AD", 0), ("AD", 1)])
                wkv, wkv_k = wload(win_v[:, :, 1024:1536], 512, wkeys("win", range(8)))
                wu, wu_k = wload(win_v[:, :, 1536:2048], 512, wkeys("win", range(8)))

                def a_load(t):
                    sl = t % 2
                    if t < NCT:
                        src = ctx_d[b, t * 128:(t + 1) * 128, :]
                    else:
                        src = x_d[b, (t - NCT) * 128:(t - NCT + 1) * 128, :]
                    DMA(XT[:, sl, :], src, w=[("XT", sl)])

                def a_n1(t):
                    norm_part1(XT[:, t % 2, :], [("XT", t % 2)], 1, 0, t % 2)

                def a_n2(t):
                    norm_part2(t % 2, t % 4)

                def a_proj(t):
                    i = t % 4
                    kvb = t % 2
                    ub = 2 + t % 2
                    for k in range(8):
                        A("pe", "matmul", bank(kvb), HT[:, k, i * 128:(i + 1) * 128], wkv[:, k, :],
                          start=(k == 0), stop=(k == 7), r=[("HT", k, i), wkv_k], w=[pb(kvb)])
                    if t >= NCT:
                        for k in range(8):
                            A("pe", "matmul", bank(ub), HT[:, k, i * 128:(i + 1) * 128], wu[:, k, :],
                              start=(k == 0), stop=(k == 7), r=[("HT", k, i), wu_k], w=[pb(ub)])

                def a_post(t):
                    kvb = t % 2
                    ub = 2 + t % 2
                    A("act", "activation", out=VV[:, t, :], in_=bank(kvb)[:, 256:512], func=AF.Copy, r=[pb(kvb)], w=[("V", t)])
                    if t >= NCT:
                        A("act", "activation", out=UU[:, t - NCT, :], in_=bank(ub), func=AF.Copy, r=[pb(ub)], w=[("U", t - NCT)])
                    return qk_norm_rope(bank(kvb)[:, 0:256], [pb(kvb)], 2, gkb, "gkb", (t - NCT) if t >= NCT else None,
                                        KT[:, :, t * 128:(t + 1) * 128], [("KT", t)], small=True, defer_b=True)

                a_load(0)
                a_load(1)
                a_n1(0)
                a_n2(0)
                for t in range(NKT):
                    if t + 1 < NKT:
                        if t + 1 == NCT:
                            DMA(AD[:], modsc[b, :, :].partition_broadcast(128), r=modsc_keys, w=[("AD", j) for j in range(6)])
                        a_n1(t + 1)
                    if t + 2 < NKT:
                        a_load(t + 2)
                    a_proj(t)
                    kb = a_post(t)
                    if t + 1 < NKT:
                        a_n2(t + 1)
                    kb()

                chk(3)
                for s in range(4):
                    def xsrc(i):
                        j = 4 * s + i
                        return x_d[b, j * 128:(j + 1) * 128, :]

                    def b_load(i):
                        DMA(XT[:, i % 2, :], xsrc(i), w=[("XT", i % 2)])

                    def stage_b1(s_):
                        def ld(i):
                            j_ = 4 * s_ + i
                            DMA(XT[:, i % 2, :], x_d[b, j_ * 128:(j_ + 1) * 128, :], w=[("XT", i % 2)])

                        def n1(i):
                            norm_part1(XT[:, i % 2, :], [("XT", i % 2)], 1, 0, i % 2)

                        def n2(i):
                            norm_part2(i % 2, i)
                        return [lambda: (ld(0), ld(1), n1(0), n1(1)),
                                lambda: (n2(0), ld(2), n1(2)),
                                lambda: (n2(1), ld(3), n1(3)),
                                lambda: n2(2),
                                lambda: n2(3)]

                    if s == 0:
                        for st_ in stage_b1(0):
                            st_()
                    wq0, wq0_k = wload(win_v[:, :, 0:512], 512, wkeys("win", range(8)))
                    wq1, wq1_k = wload(win_v[:, :, 512:1024], 512, wkeys("win", range(8)))

                    chk(4)
                    def q_proj(i):
                        pr = i % 2
                        for c_, (wq, wqk) in enumerate(((wq0, wq0_k), (wq1, wq1_k))):
                            for k in range(8):
                                A("pe", "matmul", bank(2 * pr + c_), HT[:, k, i * 128:(i + 1) * 128], wq[:, k, :],
                                  start=(k == 0), stop=(k == 7), r=[("HT", k, i), wqk], w=[pb(2 * pr + c_)])

                    def q_rope(i):
                        pr = i % 2
                        return qk_norm_rope(bank2(pr), [pb(2 * pr), pb(2 * pr + 1)], 8, gqb, "gqb", 4 * s + i,
                                            QTv[:, :, i * 128:(i + 1) * 128], [("R", h) for h in range(8)], defer_b=True)

                    gslot = [sl_ for sl_ in range(3) if ("WS", sl_) not in (wq0_k, wq1_k)][0]

                    def gates(c_):
                        wg, wg_k = wload(win_v[:, :, 2048 + c_ * 512:2048 + (c_ + 1) * 512], 512, wkeys("win", range(8)),
                                         slot=gslot)
                        for mm in range(4):
                            m = c_ * 4 + mm
                            gb_ = 4 + (m % 2)
                            for k in range(8):
                                A("pe", "matmul", bank(gb_), wg[:, k, mm * 128:(mm + 1) * 128], HT[:, k, :],
                                  start=(k == 0), stop=(k == 7), r=[("HT", k, ii) for ii in range(4)] + [wg_k], w=[pb(gb_)])
                            A("act", "activation", out=GTv[:, m, :], in_=bank(gb_), func=AF.Sigmoid, bias=bgc[:, m:m + 1],
                              scale=1.0, r=[pb(gb_), "bgc"], w=[("R", 16 + m)])

                    def pool_branch():
                        for i in range(4):
                            j = 4 * s + i
                            pbk = [0, 1, 4, 5][i]
                            for g in range(4):
                                srcs = [(j - 1, 3), (j, 1 if j == 0 else (2 if j == NLT - 1 else 0)), (j + 1, 4)]
                                srcs = [(sj, v) for sj, v in srcs if 0 <= sj < NLT]
                                for n_, (sj, v) in enumerate(srcs):
                                    A("pe", "matmul", bank(pbk)[:, g * 128:(g + 1) * 128], UU[:, sj, g * 128:(g + 1) * 128],
                                      bands[:, g * 5 + v, :], start=(n_ == 0), stop=(n_ == len(srcs) - 1),
                                      r=[("U", sj), "bands"], w=[pb(pbk)])
                            A("act", "activation", out=PDv[:, :, i * 128:(i + 1) * 128],
                              in_=bank(pbk).rearrange("p (g t) -> p g t", t=128), func=AF.Copy, r=[pb(pbk)], w=PTK)
                        for g in range(4):
                            gb_ = 4 + g % 2
                            A("pe", "matmul", bank(gb_), wgrp[:, g, :], PDv[:, g, :], start=True, stop=True,
                              r=["wgrp"] + PTK, w=[pb(gb_)])
                            A("act", "activation", out=PO[:, g, :], in_=bank(gb_), func=AF.Copy, scale=psc[:, g:g + 1],
                              r=[pb(gb_), "psc"], w=[("PO", g)])

                    q_proj(0)
                    b0 = q_rope(0)
                    q_proj(1)
                    gates(0)
                    b0()
                    b1 = q_rope(1)
                    q_proj(2)
                    gates(1)
                    b1()
                    b2 = q_rope(2)
                    q_proj(3)
                    gates(2)
                    b2()
                    b3 = q_rope(3)
                    gates(3)
                    pool_branch()
                    b3()
                    chk(4)
                    chk(5)
                    chk(6)
                    steps = [(h, pr) for h in range(NQH) for pr in range(NKT // 2)]
                    sc = 1.0 / math.sqrt(HD)

                    def qk(idx):
                        h, pr = steps[idx]
                        g = h // 4
                        slot = idx % 2
                        for u in range(2):
                            kt = 2 * pr + u
                            A("pe", "matmul", bank(2 * slot + u), KT[:, g, kt * 128:(kt + 1) * 128], QTv[:, h, :],
                              start=True, stop=True, r=[("KT", kt), ("R", h)], w=[pb(2 * slot + u)])

                    def ex(idx):
                        slot = idx % 2
                        A("act", "activation", out=PT[:, slot, :], in_=bank2(slot), func=AF.Exp, scale=sc,
                          r=[pb(2 * slot), pb(2 * slot + 1)], w=[("PT", slot)])

                    def pv(idx):
                        h, pr = steps[idx]
                        g = h // 4
                        slot = idx % 2
                        ob = 4 + h % 2
                        db = 6 + h % 2
                        for u in range(2):
                            kt = 2 * pr + u
                            first = (pr == 0 and u == 0)
                            last = (pr == NKT // 2 - 1 and u == 1)
                            A("pe", "matmul", bank(ob), VV[:, kt, g * 128:(g + 1) * 128], PT[:, slot, u * 512:(u + 1) * 512],
                              start=first, stop=last, r=[("V", kt), ("PT", slot)], w=[pb(ob)])
                            A("pe", "matmul", bank(db), onesb[:], PT[:, slot, u * 512:(u + 1) * 512],
                              start=first, stop=last, r=["onesb", ("PT", slot)], w=[pb(db)])
                        if pr == NKT // 2 - 1:
                            A("dve", "reciprocal", out=RD[:], in_=bank(db), r=[pb(db)], w=["RD0", "RD1"])
                            A("dve", "tensor_tensor", out=AOv[:, h, :], in0=bank(ob), in1=RD[:], op=ALU.mult,
                              r=[pb(ob), "RD0", "RD1"], w=[("R", 8 + h)])

                    qk(0)
                    for idx in range(len(steps)):
                        if idx + 1 < len(steps):
                            qk(idx + 1)
                        ex(idx)
                        pv(idx)

                    chk(7)
                    chk(8)
                    wpu, wpu_k = wload(wpu_v[:, :, :], 1024, wkeys("wpu", range(4)))
                    wau_c, wau_ck = wload(wau_v[:, :, 0:512], 512, wkeys("wau", range(8)))
                    for m in range(8):
                        c_, mm = m // 4, m % 4
                        if m == 4:
                            wau_c, wau_ck = wload(wau_v[:, :, 512:1024], 512, wkeys("wau", range(8)))
                        par = m % 2
                        yab, ypb = 2 * par, 2 * par + 1
                        for h in range(8):
                            A("pe", "matmul", bank(yab), wau_c[:, h, mm * 128:(mm + 1) * 128], AOv[:, h, :],
                              start=(h == 0), stop=(h == 7), r=[("R", 8 + h), wau_ck], w=[pb(yab)])
                        for g in range(4):
                            A("pe", "matmul", bank(ypb), wpu[:, g, m * 128:(m + 1) * 128], PO[:, g, :],
                              start=(g == 0), stop=(g == 3), r=[("PO", g), wpu_k], w=[pb(ypb)])
                        t1 = TF[:, par, 0:512]
                        t2 = TF[:, par, 512:1024]
                        A("dve", "tensor_tensor", out=t1, in0=bank(yab), in1=GTv[:, m, :], op=ALU.mult,
                          r=[pb(yab), ("R", 16 + m)], w=[("TF", par)])
                        A("dve", "tensor_tensor", out=t2, in0=bank(ypb), in1=GTv[:, 8 + m, :], op=ALU.mult,
                          r=[pb(ypb), ("R", 24 + m)], w=[("TF", par)])
                        A("pool", "tensor_tensor", out=ZZ[:, m, :], in0=t1, in1=t2, op=ALU.add,
                          r=[("TF", par)], w=[("Z", m, ii) for ii in range(4)])

                    wo0, wo0_k = wload(wo_v[:, :, 0:512], 512, wkeys("wo", range(8)))
                    wo1, wo1_k = wload(wo_v[:, :, 512:1024], 512, wkeys("wo", range(8)))
                    for i in range(4):
                        pr = i % 2
                        sl = i % 2
                        DMA(XT[:, sl, :], xsrc(i), w=[("XT", sl)])
                        for c_, (wo_, wok) in enumerate(((wo0, wo0_k), (wo1, wo1_k))):
                            for k in range(8):
                                A("pe", "matmul", bank(2 * pr + c_), ZZ[:, k, i * 128:(i + 1) * 128], wo_[:, k, :],
                                  start=(k == 0), stop=(k == 7), r=[("Z", k, i), wok], w=[pb(2 * pr + c_)])
                        pk = [pb(2 * pr), pb(2 * pr + 1)]
                        ss, ssk = stat(1)
                        A("act", "activation", out=JK[:], in_=bank2(pr), func=AF.Square, accum_out=ss, r=pk, w=ssk + ["JK"])
                        rs, rsk = rstd_from_ss(ss, ssk, D)
                        A("dve", "scalar_tensor_tensor", out=TF[:, sl, :], in0=bank2(pr), scalar=rs, in1=AD[:, 2, :],
                          op0=ALU.mult, op1=ALU.mult, r=pk + [("AD", 2)] + rsk, w=[("TF", sl)])
                        A("dve", "tensor_tensor", out=X1[:, i, :], in0=TF[:, sl, :], in1=XT[:, sl, :], op=ALU.add,
                          r=[("TF", sl), ("XT", sl)], w=[("X1", i)])
                        norm_part1(X1[:, i, :], [("X1", i)], 4, 3, sl)
                        if i >= 1:
                            norm_part2((i - 1) % 2, i - 1)
                    norm_part2(3 % 2, 3)
                    chk(9)
                    aTv = RR
                    for c_ in range(8):
                        w1c, w1c_k = wload(w1_v[:, :, c_ * 512:(c_ + 1) * 512], 512, wkeys("w1", range(8)))
                        for ff in range(4):
                            f = 4 * c_ + ff
                            fb = 4 + f % 2
                            for k in range(8):
                                A("pe", "matmul", bank(fb), w1c[:, k, ff * 128:(ff + 1) * 128], HT[:, k, :],
                                  start=(k == 0), stop=(k == 7), r=[("HT", k, ii) for ii in range(4)] + [w1c_k], w=[pb(fb)])
                            A("act", "activation", out=RL[:, f % 2, :], in_=bank(fb), func=AF.Relu, r=[pb(fb)], w=[("RL", f % 2)])
                            A("pool" if f % 2 == 0 else "dve", "tensor_tensor", out=aTv[:, f, :], in0=RL[:, f % 2, :],
                              in1=RL[:, f % 2, :], op=ALU.mult, r=[("RL", f % 2)], w=[("R", f)])
                    chk(10)
                    ssA, ssAk = stat(4)
                    ssB, ssBk = stat(4)
                    b1_steps = None
                    for half in range(2):
                        for pc in range(4):
                            w2p, w2p_k = wload(w2_v[:, 8 * pc:8 * pc + 8, half * 512:(half + 1) * 512], 512,
                                               wkeys("w2", range(8 * pc, 8 * pc + 8)))
                            for i in range(4):
                                for fk in range(8):
                                    f = 8 * pc + fk
                                    A("pe", "matmul", bank(i), aTv[:, f, i * 128:(i + 1) * 128], w2p[:, fk, :],
                                      start=(pc == 0 and fk == 0), stop=(pc == 3 and fk == 7), r=[("R", f), w2p_k], w=[pb(i)])
                            if half == 1 and b1_steps:
                                b1_steps[pc + 1]()
                        if half == 0:
                            for i in range(4):
                                A("dve", "tensor_copy", out=YAv[:, i, :], in_=bank(i), r=[pb(i)], w=ya_keys(i))
                                A("act", "activation", out=JK[:, 0:512], in_=YAv[:, i, :], func=AF.Square,
                                  accum_out=ssA[:, i:i + 1], r=ya_keys(i), w=[ssAk[i], "JK"])
                            chk(11)
                            b1_steps = stage_b1(s + 1) if s < 3 else None
                            if b1_steps:
                                b1_steps[0]()
                        else:
                            for i in range(4):
                                j = 4 * s + i
                                sl = i % 2
                                A("act", "activation", out=JK[:, 0:512], in_=bank(i), func=AF.Square, accum_out=ssB[:, i:i + 1],
                                  r=[pb(i)], w=[ssBk[i], "JK"])
                                A("dve", "tensor_tensor", out=ssB[:, i:i + 1], in0=ssB[:, i:i + 1], in1=ssA[:, i:i + 1], op=ALU.add,
                                  r=[ssAk[i], ssBk[i]], w=[ssBk[i]])
                                rs, rsk = rstd_from_ss(ssB[:, i:i + 1], [ssBk[i]], D)
                                A("dve", "scalar_tensor_tensor", out=TF[:, sl, 0:512], in0=YAv[:, i, :], scalar=rs,
                                  in1=AD[:, 5, 0:512], op0=ALU.mult, op1=ALU.mult, r=ya_keys(i) + rsk + [("AD", 5)], w=[("TF", sl)])
                                A("dve", "scalar_tensor_tensor", out=TF[:, sl, 512:1024], in0=bank(i), scalar=rs,
                                  in1=AD[:, 5, 512:1024], op0=ALU.mult, op1=ALU.mult, r=[pb(i), ("AD", 5)] + rsk, w=[("TF", sl)])
                                A("pool", "tensor_tensor", out=X1[:, i, :], in0=X1[:, i, :], in1=TF[:, sl, :], op=ALU.add,
                                  r=[("X1", i), ("TF", sl)], w=[("X1", i)])
                                DMA(out_d[b, j * 128:(j + 1) * 128, :], X1[:, i, :], r=[("X1", i)], w=[("out", b, j)])
                    chk(12)

        except _Stop:
            pass
        emit_program(nc, T, st)
    return nc


_CACHE = {}


def _get_program(NB):
    if NB not in _CACHE:
        _CACHE[NB] = build_program(NB)
    return _CACHE[NB]


def kernel(x, c, ctx, c_ctx, w_mod, b_mod, g_pre_mix, g_post_mix, g_pre_mlp, g_post_mlp,
           w_in, b_gate, g_q, g_k, w_attn_up, w_pool_grp, pool_scale, w_pool_up, w_out,
           w_ff1, w_ff2):
    f = lambda a: np.ascontiguousarray(np.asarray(a, dtype=np.float32))
    x = f(x); c = f(c); ctx = f(ctx)
    B = x.shape[0]
    NB = B // N_CORES
    nc = _get_program(NB)
    shared = {
        "c_ctx": f(c_ctx), "w_mod": f(w_mod)[0], "b_mod": f(b_mod)[0],
        "g_pre_mix": f(g_pre_mix)[0], "g_post_mix": f(g_post_mix)[0],
        "g_pre_mlp": f(g_pre_mlp)[0], "g_post_mlp": f(g_post_mlp)[0],
        "w_in": f(w_in)[0], "b_gate": f(b_gate)[0], "g_q": f(g_q)[0], "g_k": f(g_k)[0],
        "w_attn_up": f(w_attn_up)[0], "w_pool_grp": f(w_pool_grp)[0].reshape(512, 128),
        "pool_scale": f(pool_scale)[0], "w_pool_up": f(w_pool_up)[0], "w_out": f(w_out)[0],
        "w_ff1": f(w_ff1)[0], "w_ff2": f(w_ff2)[0],
        "k_ident": np.eye(128, dtype=np.float32).astype(ml_dtypes.bfloat16),
        "k_bands": _band_consts(),
    }
    in_maps = []
    for i in range(N_CORES):
        m = dict(shared)
        m["x"] = x[i * NB:(i + 1) * NB]
        m["c"] = c[i * NB:(i + 1) * NB]
        m["ctx"] = ctx[i * NB:(i + 1) * NB]
        in_maps.append(m)
    res = run_bass_kernel_spmd(nc, in_maps, core_ids=list(range(N_CORES)))
    out = np.concatenate([np.asarray(r["out"]) for r in res.results], axis=0)
    return out.astype(np.float32)
```
